# Optimizing a Trainium2 kernel written in Bass

```python
import math
import jax
import jax.numpy as jnp
from jax import lax
import numpy as np

D_MODEL = 1024
BATCH = 4
SEQ = 8192
DEPTH = 4
DEC_BATCH = 16
DEC_SEQ = 4096
PAST_LEN = 128

POOL_GROUPS = 4
POOL_GROUP_WIDTH = D_MODEL // 8
POOL_WIDTH = POOL_GROUPS * POOL_GROUP_WIDTH
POOL_WINDOWS = (2, 4, 8, 16)
DA_HEADS = 8
DA_HEAD_DIM = 64
DA_V_DIM = 2 * DA_HEAD_DIM
DA_WIDTH = DA_HEADS * DA_V_DIM
DA_QK_WIDTH = DA_HEADS * 2 * DA_HEAD_DIM
ROT_DIM = DA_HEAD_DIM // 4
ROPE_THETA = 500000.0
QBLOCK = 128
SUBLN_EPS = 1e-5
N_MEM = 256
XA_HEADS = 4
XA_HEAD_DIM = 128
XA_WIDTH = XA_HEADS * XA_HEAD_DIM
N_BRANCHES = 3
LN_EPS = 1e-5
IN_SPLITS = (POOL_WIDTH, POOL_WIDTH, DA_QK_WIDTH, DA_QK_WIDTH, DA_WIDTH, DA_WIDTH,
             XA_WIDTH, XA_WIDTH, N_BRANCHES * D_MODEL)
IN_COLS = sum(IN_SPLITS)
SPLIT_POINTS = [sum(IN_SPLITS[:i + 1]) for i in range(len(IN_SPLITS) - 1)]
DEEPNORM_ALPHA = (2.0 * DEPTH) ** 0.25
DEEPNORM_BETA = (8.0 * DEPTH) ** -0.25

kernel_name = "hybrid_pool_diffattn_memxattn_encoder"


def _layernorm(x, g, b):
    xf = x.astype(jnp.float32)
    mu = jnp.mean(xf, axis=-1, keepdims=True)
    var = jnp.mean(jnp.square(xf - mu), axis=-1, keepdims=True)
    y = (xf - mu) * lax.rsqrt(var + LN_EPS) * g.astype(jnp.float32) + b.astype(jnp.float32)
    return y.astype(x.dtype)


def _rope_tables(seq):
    pos = jnp.arange(seq, dtype=jnp.float32)
    inv = ROPE_THETA ** (-jnp.arange(0, ROT_DIM, 2, dtype=jnp.float32) / ROT_DIM)
    ang = pos[:, None] * inv[None, :]
    return jnp.cos(ang), jnp.sin(ang)


def _apply_partial_rope(x, cos, sin):
    half = ROT_DIM // 2
    c = cos[None, :, None, None, :].astype(x.dtype)
    s = sin[None, :, None, None, :].astype(x.dtype)
    x1 = x[..., :half]
    x2 = x[..., half:ROT_DIM]
    rot = jnp.concatenate([x1 * c - x2 * s, x2 * c + x1 * s], axis=-1)
    return jnp.concatenate([rot, x[..., ROT_DIM:]], axis=-1)


def _pool_mixer(u, pool_w, pool_scale):
    B, S, _ = u.shape
    uf = u.astype(jnp.float32)
    cs = jnp.concatenate([jnp.zeros((B, 1, POOL_WIDTH), jnp.float32), jnp.cumsum(uf, axis=1)], axis=1)
    t = jnp.arange(S)
    outs = []
    for g, w in enumerate(POOL_WINDOWS):
        lo = jnp.maximum(t - w // 2, 0)
        hi = jnp.minimum(t + w // 2 - 1, S - 1)
        csg = cs[..., g * POOL_GROUP_WIDTH:(g + 1) * POOL_GROUP_WIDTH]
        wsum = jnp.take(csg, hi + 1, axis=1) - jnp.take(csg, lo, axis=1)
        cnt = (hi - lo + 1).astype(jnp.float32)[None, :, None]
        outs.append(wsum / cnt)
    pooled = jnp.stack(outs, axis=2)
    d = (pooled - uf.reshape(B, S, POOL_GROUPS, POOL_GROUP_WIDTH)).astype(u.dtype)
    y = jnp.einsum('bsgc,gcd->bsgd', d, pool_w)
    return y.reshape(B, S, POOL_WIDTH) * pool_scale


def _diff_attention(q, k, v, lam, lam_init, subln_g):
    B, S, H, _, DH = q.shape
    nb = S // QBLOCK
    qb = q.reshape(B, nb, QBLOCK, H, 2, DH).transpose(1, 0, 2, 3, 4, 5)
    scale = DH ** -0.5

    def one_block(qblk):
        s = jnp.einsum('bqhcd,bkhcd->bhcqk', qblk, k).astype(jnp.float32) * scale
        p = jax.nn.softmax(s, axis=-1)
        a = p[:, :, 0] - lam * p[:, :, 1]
        return jnp.einsum('bhqk,bkhe->bqhe', a.astype(v.dtype), v)

    o = lax.map(one_block, qb)
    o = o.transpose(1, 0, 2, 3, 4).reshape(B, S, H, DA_V_DIM)
    of = o.astype(jnp.float32)
    of = of * lax.rsqrt(jnp.mean(jnp.square(of), axis=-1, keepdims=True) + SUBLN_EPS)
    of = of * subln_g.astype(jnp.float32) * (1.0 - lam_init)
    return of.astype(v.dtype)


def _layer(x, mem, cos, sin, lam_init, w_in, w_mem_kv, pool_w, pool_scale,
           lam_q1, lam_k1, lam_q2, lam_k2, subln_g, w_br_a, w_br_b, w_br_c, w_out, ln_g, ln_b):
    B, S, _ = x.shape
    h = jnp.einsum('bsd,de->bse', x, w_in)
    pu, pz, q, k, v, az, xq, xz, gl = jnp.split(h, SPLIT_POINTS, axis=-1)

    ya = _pool_mixer(pu, pool_w, pool_scale) * jax.nn.silu(pz)
    ya = jnp.einsum('bsc,cd->bsd', ya, w_br_a)

    q = _apply_partial_rope(q.reshape(B, S, DA_HEADS, 2, DA_HEAD_DIM), cos, sin)
    k = _apply_partial_rope(k.reshape(B, S, DA_HEADS, 2, DA_HEAD_DIM), cos, sin)
    v = v.reshape(B, S, DA_HEADS, DA_V_DIM)
    lam = (jnp.exp(jnp.sum(lam_q1.astype(jnp.float32) * lam_k1.astype(jnp.float32)))
           - jnp.exp(jnp.sum(lam_q2.astype(jnp.float32) * lam_k2.astype(jnp.float32)))
           + lam_init)
    yb = _diff_attention(q, k, v, lam, lam_init, subln_g).reshape(B, S, DA_WIDTH) * jax.nn.silu(az)
    yb = jnp.einsum('bsc,cd->bsd', yb, w_br_b)

    kv = jnp.einsum('bmd,de->bme', mem, w_mem_kv)
    km, vm = jnp.split(kv, 2, axis=-1)
    km = km.reshape(B, N_MEM, XA_HEADS, XA_HEAD_DIM)
    vm = vm.reshape(B, N_MEM, XA_HEADS, XA_HEAD_DIM)
    xq = xq.reshape(B, S, XA_HEADS, XA_HEAD_DIM)
    sc = jnp.einsum('bshd,bmhd->bhsm', xq, km).astype(jnp.float32) * (XA_HEAD_DIM ** -0.5)
    pm = jax.nn.softmax(sc, axis=-1).astype(vm.dtype)
    yc = jnp.einsum('bhsm,bmhd->bshd', pm, vm).reshape(B, S, XA_WIDTH) * jax.nn.silu(xz)
    yc = jnp.einsum('bsc,cd->bsd', yc, w_br_c)

    g = jax.nn.sigmoid(gl).reshape(B, S, N_BRANCHES, D_MODEL)
    merged = g[:, :, 0] * ya + g[:, :, 1] * yb + g[:, :, 2] * yc
    out = jnp.einsum('bsd,de->bse', merged, w_out)
    return _layernorm(DEEPNORM_ALPHA * x + out, ln_g, ln_b)


def _trunk(x, mem, ln_in_g, ln_in_b, w_in, w_mem_kv, pool_w, pool_scale,
           lam_q1, lam_k1, lam_q2, lam_k2, subln_g, w_br_a, w_br_b, w_br_c, w_out, ln_g, ln_b):
    cos, sin = _rope_tables(x.shape[1])
    x = _layernorm(x, ln_in_g, ln_in_b)
    for i in range(DEPTH):
        lam_init = 0.8 - 0.6 * math.exp(-0.3 * i)
        x = _layer(x, mem, cos, sin, lam_init, w_in[i], w_mem_kv[i], pool_w[i], pool_scale[i],
                   lam_q1[i], lam_k1[i], lam_q2[i], lam_k2[i], subln_g[i],
                   w_br_a[i], w_br_b[i], w_br_c[i], w_out[i], ln_g[i], ln_b[i])
    return x


def setup_inputs(seed: int = 0) -> dict:
    key = jax.random.key(seed)
    ks = jax.random.split(key, 24)
    f32 = jnp.float32
    x_prompt = jax.random.normal(ks[0], (BATCH, SEQ, D_MODEL), f32)
    x_sample = jax.random.normal(ks[1], (DEC_BATCH, DEC_SEQ, D_MODEL), f32)
    mem_prompt = jax.random.normal(ks[2], (BATCH, N_MEM, D_MODEL), f32)
    mem_sample = jax.random.normal(ks[3], (DEC_BATCH, N_MEM, D_MODEL), f32)
    ln_in_g = 1.0 + 0.02 * jax.random.normal(ks[4], (D_MODEL,), f32)
    ln_in_b = 0.02 * jax.random.normal(ks[5], (D_MODEL,), f32)
    col_scale = jnp.concatenate([
        jnp.ones((sum(IN_SPLITS[:4]),), f32),
        jnp.full((DA_WIDTH,), DEEPNORM_BETA, f32),
        jnp.ones((sum(IN_SPLITS[5:]),), f32)])
    w_in = jax.random.normal(ks[6], (DEPTH, D_MODEL, IN_COLS), f32) * (D_MODEL ** -0.5) * col_scale
    mem_scale = jnp.concatenate([jnp.ones((XA_WIDTH,), f32), jnp.full((XA_WIDTH,), DEEPNORM_BETA, f32)])
    w_mem_kv = jax.random.normal(ks[7], (DEPTH, D_MODEL, 2 * XA_WIDTH), f32) * (D_MODEL ** -0.5) * mem_scale
    pool_w = jax.random.normal(ks[8], (DEPTH, POOL_GROUPS, POOL_GROUP_WIDTH, POOL_GROUP_WIDTH), f32) * (POOL_GROUP_WIDTH ** -0.5)
    pool_scale = 1.0 + 0.1 * jax.random.normal(ks[9], (DEPTH, POOL_WIDTH), f32)
    lam_q1 = 0.1 * jax.random.normal(ks[10], (DEPTH, DA_HEAD_DIM), f32)
    lam_k1 = 0.1 * jax.random.normal(ks[11], (DEPTH, DA_HEAD_DIM), f32)
    lam_q2 = 0.1 * jax.random.normal(ks[12], (DEPTH, DA_HEAD_DIM), f32)
    lam_k2 = 0.1 * jax.random.normal(ks[13], (DEPTH, DA_HEAD_DIM), f32)
    subln_g = 1.0 + 0.02 * jax.random.normal(ks[14], (DEPTH, DA_V_DIM), f32)
    w_br_a = jax.random.normal(ks[15], (DEPTH, POOL_WIDTH, D_MODEL), f32) * (POOL_WIDTH ** -0.5) * DEEPNORM_BETA
    w_br_b = jax.random.normal(ks[16], (DEPTH, DA_WIDTH, D_MODEL), f32) * (DA_WIDTH ** -0.5) * DEEPNORM_BETA
    w_br_c = jax.random.normal(ks[17], (DEPTH, XA_WIDTH, D_MODEL), f32) * (XA_WIDTH ** -0.5) * DEEPNORM_BETA
    w_out = jax.random.normal(ks[18], (DEPTH, D_MODEL, D_MODEL), f32) * (D_MODEL ** -0.5) * DEEPNORM_BETA
    ln_g = 1.0 + 0.02 * jax.random.normal(ks[19], (DEPTH, D_MODEL), f32)
    ln_b = 0.02 * jax.random.normal(ks[20], (DEPTH, D_MODEL), f32)
    return {"x_prompt": x_prompt, "x_sample": x_sample, "mem_prompt": mem_prompt, "mem_sample": mem_sample,
            "ln_in_g": ln_in_g, "ln_in_b": ln_in_b, "w_in": w_in, "w_mem_kv": w_mem_kv,
            "pool_w": pool_w, "pool_scale": pool_scale, "lam_q1": lam_q1, "lam_k1": lam_k1,
            "lam_q2": lam_q2, "lam_k2": lam_k2, "subln_g": subln_g, "w_br_a": w_br_a,
            "w_br_b": w_br_b, "w_br_c": w_br_c, "w_out": w_out, "ln_g": ln_g, "ln_b": ln_b}


def reference(x_prompt, x_sample, mem_prompt, mem_sample, ln_in_g, ln_in_b, w_in, w_mem_kv,
              pool_w, pool_scale, lam_q1, lam_k1, lam_q2, lam_k2, subln_g, w_br_a, w_br_b,
              w_br_c, w_out, ln_g, ln_b):
    y_prompt = _trunk(x_prompt, mem_prompt, ln_in_g, ln_in_b, w_in, w_mem_kv, pool_w, pool_scale,
                      lam_q1, lam_k1, lam_q2, lam_k2, subln_g, w_br_a, w_br_b, w_br_c, w_out, ln_g, ln_b)
    y_sample = _trunk(x_sample, mem_sample, ln_in_g, ln_in_b, w_in, w_mem_kv, pool_w, pool_scale,
                      lam_q1, lam_k1, lam_q2, lam_k2, subln_g, w_br_a, w_br_b, w_br_c, w_out, ln_g, ln_b)
    return (y_prompt, y_sample)
```

```python
import math
import contextlib
import numpy as np
import ml_dtypes
import concourse.bass as bass
import concourse.mybir as mybir
from concourse.bass_utils import run_bass_kernel_spmd

F32 = mybir.dt.float32
BF16 = mybir.dt.bfloat16
AF = mybir.ActivationFunctionType
ALU = mybir.AluOpType

NCORES = 8
D = 1024
KC = 8
TT = 512
HALO = 8
NEG = -30000.0
LN_EPS = 1e-5
SUBLN_EPS = 1e-5
ROPE_THETA = 500000.0
SEM_CAP = 60000


class _Stop(Exception):
    pass


class Eng:
    def __init__(self, mk, raw, name, selfsync):
        self.mk, self.raw, self.name, self.selfsync = mk, raw, name, selfsync
        self.sem = None
        self.semid = None
        self.cnt = 0
        self.seen = {}
        self.last = None

    def wait(self, toks):
        best = {}
        for t in toks:
            if t is None:
                continue
            semid, val, owner = t
            if owner is self and not self.selfsync:
                continue
            if self.seen.get(semid, 0) >= val:
                continue
            if best.get(semid, 0) < val:
                best[semid] = val
        for semid, val in best.items():
            self.raw.wait_ge(self.mk.sems[semid], val)
            self.seen[semid] = val

    def sig(self, ins):
        if self.sem is None or self.cnt >= SEM_CAP:
            self.semid, self.sem = self.mk.newsem(self.name)
            self.cnt = 0
        self.cnt += 1
        ins.then_inc(self.sem, 1)
        self.last = (self.semid, self.cnt, self)
        return self.last


class Slot:
    def __init__(self, mk, name):
        self.mk, self.name = mk, name
        self.sem = None
        self.semid = None
        self.cnt = 0

    def inc(self, ins):
        if self.sem is None or self.cnt >= SEM_CAP:
            self.semid, self.sem = self.mk.newsem(self.name)
            self.cnt = 0
        self.cnt += 16
        ins.then_inc(self.sem, 16)
        tok = (self.semid, self.cnt, None)
        self.mk.pending[self.semid] = tok
        return tok


class Buf:
    def __init__(self, ap_src, name="", psum=False):
        self.t = ap_src
        self.name = name
        self.psum = psum
        self.wr = []
        self.rd = {}

    def __getitem__(self, k):
        return self.t[k]


class MK:
    def __init__(self, nc):
        self.nc = nc
        self.sems = []
        self.pending = {}
        self.pe = Eng(self, nc.tensor, "pe", False)
        self.act = Eng(self, nc.scalar, "act", True)
        self.dve = Eng(self, nc.vector, "dve", True)
        self.pool = Eng(self, nc.gpsimd, "pool", True)
        self.sp = Eng(self, nc.sync, "sp", True)
        self.engs = [self.pe, self.act, self.dve, self.pool, self.sp]
        self.uid = 0
        self.dead = False
        self.banks = []
        self.bi = 0

    def newsem(self, name):
        self.uid += 1
        s = self.nc.alloc_semaphore(name=f"{name}_{self.uid}")
        self.sems.append(s)
        return len(self.sems) - 1, s

    def _deps(self, reads, writes, extra, eng=None):
        w = list(extra)
        for b in reads:
            w += b.wr
            if b.psum:
                w += [t for t in b.rd.values() if t[2] is not eng]
        for b in writes:
            w += b.wr
            w += list(b.rd.values())
        return w

    def _commit(self, tok, reads, writes):
        key = tok[0]
        for b in reads:
            b.rd[key] = tok
        for b in writes:
            b.wr = [tok]
            b.rd = {}

    def op(self, eng, fn, reads=(), writes=(), extra=()):
        if self.dead:
            return None
        eng.wait(self._deps(reads, writes, extra, eng))
        ins = fn()
        tok = eng.sig(ins)
        self._commit(tok, reads, writes)
        return tok

    def mm(self, out_ap, ops, reads=(), writes=(), extra=(), start=True, stop=True, sig=True):
        pe = self.pe
        if self.dead:
            return None
        pe.wait(self._deps(reads, writes, extra))
        n = len(ops)
        ins = None
        for i, (l, r) in enumerate(ops):
            ins = self.nc.tensor.matmul(out_ap, l, r, start=(start and i == 0), stop=(stop and i == n - 1))
        if sig:
            tok = pe.sig(ins)
            self._commit(tok, reads, writes)
            return tok
        return None

    def dma(self, q, slot, out, in_, reads=(), writes=(), extra=()):
        if self.dead:
            return None
        q.wait(self._deps(reads, writes, extra))
        ins = q.raw.dma_start(out=out, in_=in_)
        tok = slot.inc(ins)
        self._commit(tok, reads, writes)
        return tok

    def dma_group(self, q, slot, items, bufs):
        if self.dead:
            return None
        q.wait(self._deps([], bufs, []))
        for out, in_ in items:
            slot.inc(q.raw.dma_start(out=out, in_=in_))
        tok = (slot.semid, slot.cnt, None)
        self._commit(tok, [], bufs)
        return tok

    def finalize(self, slot, bufs):
        tok = (slot.semid, slot.cnt, None)
        for b in bufs:
            b.wr = [tok]

    def bank(self):
        b = self.banks[self.bi % len(self.banks)]
        self.bi += 1
        return b

    def barrier(self):
        if self.dead:
            return
        toks = [e.last for e in self.engs if e.last is not None] + list(self.pending.values())
        for e in self.engs:
            e.wait(toks)


def _host_tables(UL, core):
    T = 3 * UL
    split = core >= 4
    pos = np.zeros(T, np.float32)
    for u in range(3):
        base = 0.0
        if u == 1 and not split:
            base = float(UL)
        pos[u * UL:(u + 1) * UL] = np.arange(UL, dtype=np.float32) + np.float32(base)
    inv = (np.float32(ROPE_THETA) ** (-np.arange(0, 16, 2, dtype=np.float32) / np.float32(16))).astype(np.float32)
    ang = (pos[:, None] * inv[None, :]).astype(np.float32)
    cos = np.cos(ang).astype(np.float32)
    sin = np.sin(ang).astype(np.float32)
    ropeC = np.ones((128, T), np.float32)
    ropeS = np.zeros((128, T), np.float32)
    for b in (0, 64):
        for m in range(8):
            ropeC[b + m] = cos[:, m]
            ropeC[b + 8 + m] = cos[:, m]
            ropeS[b + m] = sin[:, m]
            ropeS[b + 8 + m] = sin[:, m]
    seq_start = np.zeros(T, np.int64)
    seq_end = np.zeros(T, np.int64)
    for u in range(3):
        s, e = u * UL, (u + 1) * UL
        if u < 2 and not split:
            s, e = 0, 2 * UL
        seq_start[u * UL:(u + 1) * UL] = s
        seq_end[u * UL:(u + 1) * UL] = e
    t = np.arange(T)
    invc = np.zeros((4, T), np.float32)
    for g, w in enumerate((2, 4, 8, 16)):
        lo = np.maximum(t - w // 2, seq_start)
        hi = np.minimum(t + w // 2 - 1, seq_end - 1)
        invc[g] = (1.0 / (hi - lo + 1)).astype(np.float32)
    flags = np.zeros((128, 2), np.float32)
    flags[:, 0] = 0.0 if split else 1.0
    flags[:, 1] = NEG if split else 0.0
    return ropeC, ropeS, invc, flags


def _const_tables():
    R = np.zeros((128, 128), np.float32)
    for b in (0, 64):
        for m in range(8):
            R[b + m, b + m + 8] = -1.0
            R[b + 8 + m, b + m] = 1.0
    rT = np.ascontiguousarray(R.T).astype(ml_dtypes.bfloat16)
    ident = np.eye(128, dtype=np.float32)
    ones = np.ones((128, 128), np.float32).astype(ml_dtypes.bfloat16)
    onesm = (np.ones((128, 128), np.float32) / 128.0).astype(ml_dtypes.bfloat16)
    return rT, ident, ones, onesm


def build(UL, DEPTH):
    import os
    STOP = float(os.environ.get("MK_STOP", "99"))
    T = 3 * UL
    NT = T // TT
    TPU = UL // TT
    nc = bass.Bass("TRN2", target_bir_lowering=False)
    mk = MK(nc)
    pe, act, dve, pool, sp = mk.pe, mk.act, mk.dve, mk.pool, mk.sp

    def din(name, shape, dt=F32):
        return nc.dram_tensor(name, list(shape), dt, kind="ExternalInput").ap()

    def dscr(name, shape, dt):
        return nc.dram_tensor(name, list(shape), dt, kind="Internal").ap()

    x_in = din("x", [T, D])
    mem_in = din("mem", [3 * 256, D])
    ln_in_g = din("ln_in_g", [D])
    ln_in_b = din("ln_in_b", [D])
    w_in = din("w_in", [DEPTH, D, 9216])
    w_mem = din("w_mem_kv", [DEPTH, D, 1024])
    pool_w = din("pool_w", [DEPTH, 4, 128, 128])
    pool_scale = din("pool_scale", [DEPTH, 512])
    lamv = [din(n, [DEPTH, 64]) for n in ("lam_q1", "lam_k1", "lam_q2", "lam_k2")]
    subln_g = din("subln_g", [DEPTH, 128])
    w_br_a = din("w_br_a", [DEPTH, 512, D])
    w_br_b = din("w_br_b", [DEPTH, 1024, D])
    w_br_c = din("w_br_c", [DEPTH, 512, D])
    w_out = din("w_out", [DEPTH, D, D])
    ln_g = din("ln_g", [DEPTH, D])
    ln_b = din("ln_b", [DEPTH, D])
    ropeC_d = din("ropeC", [128, T])
    ropeS_d = din("ropeS", [128, T])
    invc_d = din("invc", [4, T])
    flags_d = din("flags", [128, 2])
    rT_d = din("rT", [128, 128], BF16)
    ident_d = din("ident", [128, 128])
    ones_d = din("ones", [128, 128], BF16)
    onesm_d = din("onesm", [128, 128], BF16)
    y_out = nc.dram_tensor("y", [T, D], F32, kind="ExternalOutput").ap()

    win_b = dscr("win_b", [DEPTH, 72, 128, KC, 128], BF16)
    wv_b = dscr("wv_b", [DEPTH, 128, KC, 1024], BF16)
    wmem_b = dscr("wmem_b", [DEPTH, 128, KC, 1024], BF16)
    poolw_b = dscr("poolw_b", [DEPTH, 128, 4, 128], BF16)
    wbra_b = dscr("wbra_b", [DEPTH, 128, 4, 1024], BF16)
    wbrb_b = dscr("wbrb_b", [DEPTH, 128, 8, 1024], BF16)
    wbrc_b = dscr("wbrc_b", [DEPTH, 128, 4, 1024], BF16)
    wout_b = dscr("wout_b", [DEPTH, 128, 8, 1024], BF16)
    xs = [dscr(f"xs{i}", [T, D], F32) for i in range(2)]
    xT = [dscr(f"xT{i}", [D, T], BF16) for i in range(2)]
    qT = dscr("qT", [D, T], BF16)
    kT = dscr("kT", [D, T], BF16)
    vv = dscr("vv", [T, D], BF16)
    ybT = dscr("ybT", [D, T], BF16)
    memT = dscr("memT", [D, 3 * 256], BF16)

    es = contextlib.ExitStack()
    cnt = [0]

    def sb(ctx, shape, dt, name="t"):
        cnt[0] += 1
        return Buf(ctx.enter_context(nc.sbuf_tensor(f"{name}{cnt[0]}", list(shape), dt)), name)

    free_slots = []
    cur_slots = []

    def slot(name):
        s_ = free_slots.pop() if free_slots else Slot(mk, name)
        cur_slots.append(s_)
        return s_

    def recycle():
        free_slots.extend(cur_slots)
        cur_slots.clear()

    def stage(n):
        if n > STOP:
            mk.dead = True

    V = nc.vector
    A = nc.scalar
    G = nc.gpsimd

    with es:
        for i in range(8):
            mk.banks.append(Buf(es.enter_context(nc.psum_tensor(f"ps{i}", [128, 512], F32)), f"ps{i}", psum=True))
        rT = sb(es, [128, 128], BF16, "rT")
        ident = sb(es, [128, 128], F32, "ident")
        ones = sb(es, [128, 128], BF16, "ones")
        onesm = sb(es, [128, 128], BF16, "onesm")
        flags = sb(es, [128, 2], F32, "flags")
        zcol = sb(es, [128, 1], F32, "zcol")
        neglam = sb(es, [128, DEPTH], F32, "neglam")
        gsub = sb(es, [128, DEPTH], F32, "gsub")
        pscale = sb(es, [128, DEPTH, 4], F32, "pscale")
        kmT = sb(es, [128, 3, 4, 256], BF16, "kmT")
        vm = sb(es, [128, 3, 2, 512], BF16, "vm")
        cs = slot("const")
        items = [(b[:], d_) for (b, d_) in ((rT, rT_d), (ident, ident_d), (ones, ones_d), (onesm, onesm_d), (flags, flags_d))]
        for l in range(DEPTH):
            items.append((gsub[:, l:l + 1], subln_g[l].rearrange("(p o) -> p o", o=1)))
            for g in range(4):
                items.append((pscale[:, l, g:g + 1], pool_scale[l, g * 128:(g + 1) * 128].rearrange("(p o) -> p o", o=1)))
        mk.dma_group(sp, cs, items, [rT, ident, ones, onesm, flags, gsub, pscale])
        mk.op(dve, lambda: V.memset(zcol[:], 0.0), writes=[zcol])
        ones32 = sb(es, [128, 128], F32, "ones32")
        mk.op(dve, lambda: V.memset(ones32[:], 1.0), writes=[ones32])
        epscol = sb(es, [128, 1], F32, "epscol")
        mk.op(dve, lambda: V.memset(epscol[:], LN_EPS), writes=[epscol])

        try:
            wslot = Slot(mk, "wcast")
            for l in range(DEPTH):
                for ch in range(72):
                    if 24 <= ch < 32:
                        continue
                    mk.dma(pool, wslot, win_b[l, ch],
                           w_in[l, :, ch * 128:(ch + 1) * 128].rearrange("(kc p) c -> p kc c", p=128))
                for kc in range(KC):
                    mk.dma(pool, wslot, wv_b[l, :, kc, :], w_in[l, kc * 128:(kc + 1) * 128, 3072:4096])
                    mk.dma(pool, wslot, wmem_b[l, :, kc, :], w_mem[l, kc * 128:(kc + 1) * 128, :])
                    mk.dma(pool, wslot, wbrb_b[l, :, kc, :], w_br_b[l, kc * 128:(kc + 1) * 128, :])
                    mk.dma(pool, wslot, wout_b[l, :, kc, :], w_out[l, kc * 128:(kc + 1) * 128, :])
                for g in range(4):
                    mk.dma(pool, wslot, wbra_b[l, :, g, :], w_br_a[l, g * 128:(g + 1) * 128, :])
                    mk.dma(pool, wslot, wbrc_b[l, :, g, :], w_br_c[l, g * 128:(g + 1) * 128, :])
                    mk.dma(pool, wslot, poolw_b[l, :, g, :], pool_w[l, g])

            stage(2)
            with contextlib.ExitStack() as ph:
                lt = [sb(ph, [128, DEPTH * 64], F32, "lam") for _ in range(4)]
                pr = sb(ph, [128, DEPTH * 64], F32, "lamp")
                s1 = sb(ph, [128, DEPTH], F32, "lams1")
                s2 = sb(ph, [128, DEPTH], F32, "lams2")
                cs2 = slot("const2")
                mk.dma_group(sp, cs2, [(lt[i][:], lamv[i].rearrange("l d -> (l d)").partition_broadcast(128)) for i in range(4)], lt)
                for (a_, b_, s_) in ((0, 1, s1), (2, 3, s2)):
                    mk.op(dve, lambda: V.tensor_tensor(pr[:], lt[a_][:], lt[b_][:], ALU.mult), reads=[lt[a_], lt[b_]], writes=[pr])
                    for l in range(DEPTH):
                        mk.op(dve, lambda: V.tensor_reduce(s_[:, l:l + 1], pr[:, l * 64:(l + 1) * 64], mybir.AxisListType.X, ALU.add), reads=[pr], writes=[s_])
                    mk.op(act, lambda: A.activation(s_[:], s_[:], AF.Exp), reads=[s_], writes=[s_])
                mk.op(dve, lambda: V.tensor_tensor(neglam[:], s2[:], s1[:], ALU.subtract), reads=[s1, s2], writes=[neglam])
                for l in range(DEPTH):
                    li = 0.8 - 0.6 * math.exp(-0.3 * l)
                    mk.op(dve, lambda: V.tensor_scalar_add(neglam[:, l:l + 1], neglam[:, l:l + 1], -li), reads=[neglam], writes=[neglam])
                    mk.op(dve, lambda: V.tensor_scalar_mul(gsub[:, l:l + 1], gsub[:, l:l + 1], 1.0 - li), reads=[gsub], writes=[gsub])
                mk.barrier()
                recycle()

            def ln_block(lnb, row0, b, z, zslot, dst_xs, want_T, gtab, btab):
                st, mv, rstd, xTo, sl_t = lnb
                mk.op(dve, lambda: V.bn_stats(st[:, 0:6], z[:, 0:512]), reads=[z], writes=[st])
                mk.op(dve, lambda: V.bn_stats(st[:, 6:12], z[:, 512:1024]), reads=[z], writes=[st])
                mk.op(dve, lambda: V.bn_aggr(mv[:, 0:2], st[:, 0:12]), reads=[st], writes=[mv])
                mk.op(act, lambda: A.activation(rstd[:], mv[:, 1:2], AF.Ln, bias=epscol[:]), reads=[mv, epscol], writes=[rstd])
                mk.op(act, lambda: A.activation(rstd[:], rstd[:], AF.Exp, scale=-0.5), reads=[rstd], writes=[rstd])
                mk.op(dve, lambda: V.tensor_scalar(z[:], z[:], mv[:, 0:1], rstd[:], ALU.subtract, ALU.mult), reads=[z, mv, rstd], writes=[z])
                mk.op(pool, lambda: G.tensor_tensor(z[:], z[:], gtab[:], ALU.mult), reads=[z, gtab], writes=[z])
                mk.op(pool, lambda: G.tensor_tensor(z[:], z[:], btab[:], ALU.add), reads=[z, btab], writes=[z])
                mk.dma(sp, zslot, dst_xs[row0: row0 + 128, :], z[:], reads=[z])
                if want_T:
                    for half in range(2):
                        if mk.dead:
                            break
                        bk = mk.bank()
                        pe.wait(mk._deps([z, ident], [bk], []))
                        ins = None
                        for f in range(4):
                            fc = half * 4 + f
                            ins = nc.tensor.transpose(bk[:, f * 128:(f + 1) * 128], z[:, fc * 128:(fc + 1) * 128], ident[:])
                        tk = pe.sig(ins)
                        mk._commit(tk, [z, ident], [bk])
                        for f in range(4):
                            fc = half * 4 + f
                            mk.op(act, lambda: A.copy(xTo[:, fc, b * 128:(b + 1) * 128], bk[:, f * 128:(f + 1) * 128]), reads=[bk], writes=[xTo])

            def ln_flush(lnb, tile_i, dst_xT):
                st, mv, rstd, xTo, sl_t = lnb
                mk.dma(sp, sl_t, dst_xT[:, tile_i * TT:(tile_i + 1) * TT].rearrange("(kc p) t -> p kc t", p=128), xTo[:], reads=[xTo])

            stage(3)
            with contextlib.ExitStack() as ph:
                gtab = sb(ph, [128, D], F32, "gtab")
                btab = sb(ph, [128, D], F32, "btab")
                cs3 = slot("const3")
                mk.dma_group(sp, cs3, [(gtab[:], ln_in_g.partition_broadcast(128)), (btab[:], ln_in_b.partition_broadcast(128))], [gtab, btab])
                lnb = (sb(ph, [128, 12], F32, "st"), sb(ph, [128, 2], F32, "mv"), sb(ph, [128, 1], F32, "rstd"),
                       sb(ph, [128, 8, TT], BF16, "xTo"), slot("p0t"))
                zr = [sb(ph, [128, D], F32, "z") for _ in range(4)]
                zls = [slot(f"p0l{i}") for i in range(4)]
                zss = [slot(f"p0s{i}") for i in range(4)]
                nblk = T // 128

                def p0_load(bi):
                    mk.dma(sp, zls[bi % 4], zr[bi % 4][:], x_in[bi * 128:(bi + 1) * 128, :], writes=[zr[bi % 4]])

                p0_load(0)
                p0_load(1)
                for bi in range(nblk):
                    if bi + 2 < nblk:
                        p0_load(bi + 2)
                    ln_block(lnb, bi * 128, bi % 4, zr[bi % 4], zss[bi % 4], xs[0], True, gtab, btab)
                    if bi % 4 == 3:
                        ln_flush(lnb, bi // 4, xT[0])
                zsets = [zr[0:2], zr[2:4]]
                zsl = [zls[0], zls[2]]
                mT = sb(ph, [128, 8, 256], BF16, "mT")
                msl = slot("p0m")
                for u in range(3):
                    zb = zsets[u % 2]
                    mk.dma_group(sp, zsl[u % 2], [(zb[mb][:], mem_in[u * 256 + mb * 128: u * 256 + (mb + 1) * 128, :]) for mb in range(2)], zb[0:2])
                    for fc in range(8):
                        if mk.dead:
                            break
                        bk = mk.bank()
                        pe.wait(mk._deps(zb[0:2], [bk], []))
                        for mb in range(2):
                            ins = nc.tensor.transpose(bk[:, mb * 128:(mb + 1) * 128], zb[mb][:, fc * 128:(fc + 1) * 128], ident[:])
                        tk = pe.sig(ins)
                        mk._commit(tk, zb[0:2], [bk])
                        mk.op(act, lambda: A.copy(mT[:, fc, :], bk[:, 0:256]), reads=[bk], writes=[mT])
                    mk.dma(sp, msl, memT[:, u * 256:(u + 1) * 256].rearrange("(kc p) m -> p kc m", p=128), mT[:], reads=[mT])
                mk.barrier()
                recycle()

            for l in range(DEPTH):
                cur, nxt = l % 2, (l + 1) % 2
                last = (l == DEPTH - 1)
                stage(4)
                with contextlib.ExitStack() as ph:
                    wqk = sb(ph, [128, 16, KC, 128], BF16, "wqk")
                    wv = sb(ph, [128, KC, 1024], BF16, "wv")
                    wm = sb(ph, [128, KC, 1024], BF16, "wm")
                    mTa = sb(ph, [128, KC, 768], BF16, "mTa")
                    ws = slot("p1w")
                    items = [(wqk[:, c8:c8 + 4], win_b[l, 8 + c8: 8 + c8 + 4].rearrange("ch p kc c -> p ch kc c")) for c8 in range(0, 16, 4)]
                    items += [(wv[:], wv_b[l]), (wm[:], wmem_b[l]), (mTa[:], memT.rearrange("(kc p) m -> p kc m", p=128))]
                    mk.dma_group(sp, ws, items, [wqk, wv, wm, mTa])
                    for u in range(3):
                        for h in range(4):
                            bk = mk.bank()
                            mk.mm(bk[:, 0:256], [(wm[:, kc, h * 128:(h + 1) * 128], mTa[:, kc, u * 256:(u + 1) * 256]) for kc in range(KC)],
                                  reads=[wm, mTa], writes=[bk])
                            mk.op(act, lambda: A.copy(kmT[:, u, h, :], bk[:, 0:256]), reads=[bk], writes=[kmT])
                        for mb in range(2):
                            bk = mk.bank()
                            mk.mm(bk[:], [(mTa[:, kc, u * 256 + mb * 128: u * 256 + (mb + 1) * 128], wm[:, kc, 512:1024]) for kc in range(KC)],
                                  reads=[wm, mTa], writes=[bk])
                            mk.op(dve, lambda: V.tensor_copy(vm[:, u, mb, :], bk[:]), reads=[bk], writes=[vm])
                    stage(4.1)
                    xts = [sb(ph, [128, KC, TT], BF16, "xt") for _ in range(2)]
                    rcs = [sb(ph, [128, TT], F32, "rc") for _ in range(2)]
                    rss = [sb(ph, [128, TT], F32, "rs") for _ in range(2)]
                    lsl = [slot("p1l0"), slot("p1l1")]
                    qraw = [sb(ph, [128, TT], BF16, "qraw") for _ in range(2)]
                    t1s = [sb(ph, [128, TT], F32, "t1") for _ in range(2)]
                    t2s = [sb(ph, [128, TT], F32, "t2") for _ in range(2)]
                    qo = [sb(ph, [128, TT], BF16, "qo") for _ in range(4)]
                    qsl = [slot(f"p1q{i}") for i in range(4)]
                    vo = [sb(ph, [128, 4, 1024], BF16, "vo") for _ in range(2)]
                    vsl = [slot("p1v0"), slot("p1v1")]

                    def p1_load(ti):
                        i = ti % 2
                        mk.dma_group(sp, lsl[i], [(xts[i][:], xT[cur][:, ti * TT:(ti + 1) * TT].rearrange("(kc p) t -> p kc t", p=128)),
                                                  (rcs[i][:], ropeC_d[:, ti * TT:(ti + 1) * TT]),
                                                  (rss[i][:], ropeS_d[:, ti * TT:(ti + 1) * TT])], [xts[i], rcs[i], rss[i]])

                    p1_load(0)
                    qi_box = [0]
                    for ti in range(NT):
                        if ti + 1 < NT:
                            p1_load(ti + 1)
                        i = ti % 2
                        xt, rc, rs = xts[i], rcs[i], rss[i]
                        items16 = [(which, h) for which in (0, 1) for h in range(8)]

                        def p1_proj(k):
                            nonlocal_qi = qi_box[0]
                            qi_box[0] += 1
                            which, h = items16[k]
                            bk = mk.bank()
                            mk.mm(bk[:], [(wqk[:, which * 8 + h, kc, :], xt[:, kc, :]) for kc in range(KC)], reads=[wqk, xt], writes=[bk])
                            qr = qraw[nonlocal_qi % 2]
                            mk.op(act, lambda: A.copy(qr[:], bk[:]), reads=[bk], writes=[qr])
                            return bk, qr, nonlocal_qi

                        cur_p = p1_proj(0)
                        for k in range(16):
                            nxt_p = p1_proj(k + 1) if k + 1 < 16 else None
                            which, h = items16[k]
                            dst = qT if which == 0 else kT
                            bk, qr, qi_ = cur_p
                            t1, t2, q_o, q_s = t1s[qi_ % 2], t2s[qi_ % 2], qo[qi_ % 4], qsl[qi_ % 4]
                            bk2 = mk.bank()
                            mk.mm(bk2[:], [(rT[:], qr[:])], reads=[rT, qr], writes=[bk2])
                            mk.op(dve, lambda: V.tensor_tensor(t1[:], bk[:], rc[:], ALU.mult), reads=[bk, rc], writes=[t1])
                            mk.op(dve, lambda: V.tensor_tensor(t2[:], bk2[:], rs[:], ALU.mult), reads=[bk2, rs], writes=[t2])
                            mk.op(pool, lambda: G.tensor_tensor(q_o[:], t1[:], t2[:], ALU.add), reads=[t1, t2], writes=[q_o])
                            mk.dma(sp, q_s, dst[h * 128:(h + 1) * 128, ti * TT:(ti + 1) * TT], q_o[:], reads=[q_o])
                            cur_p = nxt_p
                        stage(4.2)
                        v_o = vo[i]
                        for b in range(4):
                            for half in range(2):
                                bk = mk.bank()
                                mk.mm(bk[:], [(xt[:, kc, b * 128:(b + 1) * 128], wv[:, kc, half * 512:(half + 1) * 512]) for kc in range(KC)],
                                      reads=[wv, xt], writes=[bk])
                                if half == 0:
                                    mk.op(act, lambda: A.copy(v_o[:, b, 0:512], bk[:]), reads=[bk], writes=[v_o])
                                else:
                                    mk.op(dve, lambda: V.tensor_copy(v_o[:, b, 512:1024], bk[:]), reads=[bk], writes=[v_o])
                        mk.dma(sp, vsl[i], vv[ti * TT:(ti + 1) * TT, :].rearrange("(b p) n -> p b n", p=128), v_o[:], reads=[v_o])
                    mk.barrier()
                    recycle()

                stage(5)
                with contextlib.ExitStack() as ph:
                    KL = 2 * UL
                    kts = [sb(ph, [128, KL], BF16, "kt") for _ in range(2)]
                    vts = [sb(ph, [128, KL // 128, 128], BF16, "vt") for _ in range(2)]
                    kvs = [slot("p2kv0"), slot("p2kv1")]
                    qts = [[sb(ph, [128, TT], BF16, "qta"), sb(ph, [128, TT], BF16, "qtb")] for _ in range(3)]
                    for qq in qts:
                        for qb_ in qq:
                            mk.op(pool, lambda: G.memset(qb_[:], 0.0), writes=[qb_])
                    qss = [slot(f"p2q{i}") for i in range(3)]
                    NPT = 6
                    pts = [sb(ph, [128, TT], BF16, "pt") for _ in range(NPT)]
                    accs = [sb(ph, [128, TT], F32, "acc") for _ in range(6)]
                    rL = sb(ph, [128, TT], F32, "rL")
                    oc = [sb(ph, [128, TT], F32, "oc") for _ in range(2)]
                    od = sb(ph, [128, TT], F32, "od")
                    osq = sb(ph, [128, TT], BF16, "osq")
                    rst = sb(ph, [128, TT], F32, "rst")
                    ybo = [sb(ph, [128, TT], BF16, "ybo") for _ in range(2)]
                    yss = [slot("p2y0"), slot("p2y1")]
                    Sb = [mk.banks[0], mk.banks[1], mk.banks[7]]
                    Ob = [mk.banks[2], mk.banks[3]]
                    Lb = [mk.banks[4], mk.banks[5]]
                    Mb = mk.banks[6]
                    slots_ = [(0, 2 * UL, list(range(0, 2 * TPU))), (2 * UL, 3 * UL, list(range(2 * TPU, 3 * TPU)))]
                    sh = [(s_, h) for s_ in range(2) for h in range(8)]

                    def p2_loadkv(idx):
                        s_, h = sh[idx]
                        k0, k1, _ = slots_[s_]
                        i = idx % 2
                        n = k1 - k0
                        items = [(kts[i][:, 0:n], kT[h * 128:(h + 1) * 128, k0:k1])]
                        for c0 in range(0, n // 128, 8):
                            c1 = min(c0 + 8, n // 128)
                            items.append((vts[i][:, c0:c1, :],
                                          vv[k0 + c0 * 128:k0 + c1 * 128, h * 128:(h + 1) * 128].rearrange("(c p) e -> p c e", p=128)))
                        mk.dma_group(sp, kvs[i], items, [kts[i], vts[i]])

                    qjobs = [(idx, qt) for idx in range(len(sh)) for qt in slots_[sh[idx][0]][2]]

                    def p2_loadq(j):
                        idx, qt = qjobs[j]
                        h = sh[idx][1]
                        qa_, qb_ = qts[j % 3]
                        mk.dma_group(sp, qss[j % 3], [(qa_[0:64, :], qT[h * 128:h * 128 + 64, qt * TT:(qt + 1) * TT]),
                                                      (qb_[64:128, :], qT[h * 128 + 64:(h + 1) * 128, qt * TT:(qt + 1) * TT])], [qa_, qb_])

                    p2_loadkv(0)
                    p2_loadq(0)
                    p2_loadq(1)
                    gidx = 0
                    jn = 0
                    deferred = []
                    loaded_kv = 0
                    for j, (idx, qt) in enumerate(qjobs):
                        s_, h = sh[idx]
                        k0, k1, qlist = slots_[s_]
                        if qt == qlist[0] and idx + 1 < len(sh) and loaded_kv < idx + 1:
                            p2_loadkv(idx + 1)
                            loaded_kv = idx + 1
                        if j + 2 < len(qjobs):
                            p2_loadq(j + 2)
                        kt, vt, qtile = kts[idx % 2], vts[idx % 2], qts[j % 3]
                        nch = (k1 - k0) // 128
                        qunit = qt // TPU
                        for c in range(2):
                            O, L = Ob[gidx % 2], Lb[gidx % 2]
                            gidx += 1
                            r0 = c * 64

                            qpad = qtile[c]

                            def qk(jc):
                                S = Sb[(jn + jc) % 3]
                                mk.mm(S[:], [(kt[:, jc * 128:(jc + 1) * 128], qpad[:])], reads=[kt, qpad], writes=[S])

                            qk(0)
                            if nch > 1:
                                qk(1)
                            for jc in range(nch):
                                S = Sb[(jn + jc) % 3]
                                P = pts[(jn + jc) % NPT]
                                kunit = (k0 + jc * 128) // UL
                                bias = flags[:, 1:2] if (kunit != qunit) else zcol[:]
                                mk.op(act, lambda: A.activation(P[:], S[:], AF.Exp, bias=bias, scale=0.125), reads=[S, flags, zcol], writes=[P])
                                if jc + 2 < nch:
                                    qk(jc + 2)
                                lastc = (jc == nch - 1)
                                mk.mm(O[:], [(vt[:, jc, :], P[:])], reads=[vt, P], writes=[O], start=(jc == 0), stop=lastc, sig=True)
                                ab_ = ((gidx - 1) % 2) * 3
                                if jc % 3 == 2:
                                    acc_ = accs[ab_ + 2]
                                    if jc == 2:
                                        mk.op(pool, lambda: G.tensor_copy(acc_[:], P[:]), reads=[P], writes=[acc_])
                                    else:
                                        mk.op(pool, lambda: G.tensor_tensor(acc_[:], acc_[:], P[:], ALU.add), reads=[P, acc_], writes=[acc_])
                                else:
                                    dj = jc - jc // 3
                                    acc_ = accs[ab_ + (dj % 2)]
                                    if dj < 2:
                                        mk.op(dve, lambda: V.tensor_copy(acc_[:], P[:]), reads=[P], writes=[acc_])
                                    else:
                                        mk.op(dve, lambda: V.tensor_tensor(acc_[:], acc_[:], P[:], ALU.add), reads=[P, acc_], writes=[acc_])
                                while deferred and deferred[0][0] <= 0:
                                    deferred.pop(0)[1]()
                                for dfe in deferred:
                                    dfe[0] -= 1
                            jn += nch
                            ocb = oc[c]
                            aA, aB, aC = accs[((gidx - 1) % 2) * 3: ((gidx - 1) % 2) * 3 + 3]
                            mk.op(dve, lambda: V.tensor_tensor(aA[:], aA[:], aB[:], ALU.add), reads=[aA, aB], writes=[aA])
                            mk.op(dve, lambda: V.tensor_tensor(aA[:], aA[:], aC[:], ALU.add), reads=[aA, aC], writes=[aA])
                            mk.mm(L[:], [(ones32[:], aA[:])], reads=[ones32, aA], writes=[L])
                            mk.op(dve, lambda: V.reciprocal(rL[:], L[:]), reads=[L], writes=[rL])
                            mk.op(dve, lambda: V.tensor_tensor(ocb[:], O[:], rL[:], ALU.mult), reads=[O, rL], writes=[ocb])
                        yb_o, y_s = ybo[j % 2], yss[j % 2]
                        mk.op(dve, lambda: V.scalar_tensor_tensor(od[:], oc[1][:], neglam[:, l:l + 1], oc[0][:], ALU.mult, ALU.add),
                              reads=[oc[0], oc[1], neglam], writes=[od])
                        mk.op(act, lambda: A.activation(osq[:], od[:], AF.Square), reads=[od], writes=[osq])

                        def part_b(h=h, qt=qt, yb_o=yb_o, y_s=y_s):
                            mk.mm(Mb[:], [(onesm[:], osq[:])], reads=[onesm, osq], writes=[Mb])
                            mk.op(act, lambda: A.activation(rst[:], Mb[:], AF.Ln, bias=epscol[:]), reads=[Mb, epscol], writes=[rst])
                            mk.op(act, lambda: A.activation(rst[:], rst[:], AF.Exp, scale=-0.5), reads=[rst], writes=[rst])
                            mk.op(dve, lambda: V.scalar_tensor_tensor(yb_o[:], od[:], gsub[:, l:l + 1], rst[:], ALU.mult, ALU.mult),
                                  reads=[od, gsub, rst], writes=[yb_o])
                            mk.dma(sp, y_s, ybT[h * 128:(h + 1) * 128, qt * TT:(qt + 1) * TT], yb_o[:], reads=[yb_o])

                        deferred.append([3, part_b])
                    while deferred:
                        deferred.pop(0)[1]()
                    mk.barrier()
                    recycle()

                stage(6)
                with contextlib.ExitStack() as ph:
                    wbra = sb(ph, [128, 4, 1024], BF16, "wbra")
                    wbrb = sb(ph, [128, 8, 1024], BF16, "wbrb")
                    wbrc = sb(ph, [128, 4, 1024], BF16, "wbrc")
                    wout = sb(ph, [128, 8, 1024], BF16, "wout")
                    poolw = sb(ph, [128, 4, 128], BF16, "poolw")
                    gtab = sb(ph, [128, D], F32, "gtab")
                    btab = sb(ph, [128, D], F32, "btab")
                    ws = slot("p3w")
                    mk.dma_group(sp, ws, [(wbra[:], wbra_b[l]), (wbrb[:], wbrb_b[l]), (wbrc[:], wbrc_b[l]), (wout[:], wout_b[l]),
                                          (poolw[:], poolw_b[l]), (gtab[:], ln_g[l].partition_broadcast(128)),
                                          (btab[:], ln_b[l].partition_broadcast(128))], [wbra, wbrb, wbrc, wout, poolw, gtab, btab])
                    NB = 6
                    ring = [sb(ph, [128, KC, 128], BF16, "ring") for _ in range(NB)]
                    rsl = [slot(f"p3r{i}") for i in range(NB)]
                    order = [0, 1, 2, 3, 4, 5, 6, 7] + list(range(32, 48))
                    for dc in range(8):
                        order += [48 + dc, 56 + dc, 64 + dc]
                    NS = len(order)
                    total = NT * NS
                    issued = [0]

                    def ring_fill(upto):
                        while issued[0] < min(upto, total):
                            s_ = issued[0]
                            ch = order[s_ % NS]
                            mk.dma(sp, rsl[s_ % NB], ring[s_ % NB][:], win_b[l, ch], writes=[ring[s_ % NB]])
                            issued[0] += 1

                    sq = [0]

                    def wchunk():
                        s_ = sq[0]
                        ring_fill(s_ + NB)
                        sq[0] += 1
                        return ring[s_ % NB]

                    XW = TT + 2 * HALO
                    xts = [sb(ph, [128, KC, XW], BF16, "xt") for _ in range(2)]
                    ybs = [sb(ph, [128, 8, TT], BF16, "yb") for _ in range(2)]
                    ics = [sb(ph, [128, 4, TT], F32, "ic") for _ in range(1)]
                    lsl = [slot("p3l0"), slot("p3l1")]
                    icsl = slot("p3ic")
                    xres = [sb(ph, [128, D], F32, "xres") for _ in range(1)]
                    xrs = [slot("p3x0")]
                    zb = [sb(ph, [128, D], F32, "z") for _ in range(2)]
                    zss = [slot("p3z0"), slot("p3z1")]
                    lnb = (sb(ph, [128, 12], F32, "st"), sb(ph, [128, 2], F32, "mv"), sb(ph, [128, 1], F32, "rstd"),
                           sb(ph, [128, 8, TT], BF16, "xTo"), slot("p3t"))
                    U = [sb(ph, [128, XW], F32, "U") for _ in range(4)]
                    w2 = sb(ph, [128, XW], F32, "w2")
                    w4 = sb(ph, [128, XW], F32, "w4")
                    w8 = sb(ph, [128, XW], F32, "w8")
                    w16 = sb(ph, [128, XW], F32, "w16")
                    dT = [sb(ph, [128, TT], BF16, "dT") for _ in range(2)]
                    NSIL = 2
                    sil = [sb(ph, [128, TT], F32, "sil") for _ in range(NSIL)]
                    yain = sb(ph, [128, 4, TT], BF16, "yain")
                    ycin = sb(ph, [128, 4, TT], BF16, "ycin")
                    pm = [sb(ph, [128, TT], BF16, "pm") for _ in range(2)]
                    rl3 = sb(ph, [128, TT], F32, "rl3")
                    NGT = 3
                    gt = [sb(ph, [128, TT], F32, "gt") for _ in range(NGT)]
                    ma = [sb(ph, [128, TT], F32, "ma") for _ in range(3)]
                    mrg = sb(ph, [128, 8, TT], BF16, "mrg")
                    dst_xs = y_out if last else xs[nxt]
                    dst_xT = None if last else xT[nxt]

                    def p3_load(ti):
                        i = ti % 2
                        t0 = ti * TT
                        lo = max(t0 - HALO, 0)
                        hi = min(t0 + TT + HALO, T)
                        c0 = lo - (t0 - HALO)
                        mk.dma_group(sp, lsl[i], [(xts[i][:, :, c0:c0 + (hi - lo)], xT[cur][:, lo:hi].rearrange("(kc p) t -> p kc t", p=128)),
                                                  (ybs[i][:], ybT[:, t0:t0 + TT].rearrange("(h p) t -> p h t", p=128))], [xts[i], ybs[i]])

                    for i in range(2):
                        mk.op(pool, lambda: G.memset(xts[i][:], 0.0), writes=[xts[i]])
                    p3_load(0)
                    gi = 0
                    si = 0
                    for ti in range(NT):
                        if ti + 1 < NT:
                            p3_load(ti + 1)
                        i = ti % 2
                        mk.dma_group(sp, icsl, [(ics[0][:, g, :], invc_d[g, ti * TT:(ti + 1) * TT].partition_broadcast(128)) for g in range(4)], [ics[0]])
                        xt, ybt, ic = xts[i], ybs[i], ics[0]
                        unit = ti // TPU
                        first_in_unit = (ti % TPU == 0)
                        last_in_unit = (ti % TPU == TPU - 1)
                        xm = lambda kc: xt[:, kc, HALO:HALO + TT]
                        for g in range(4):
                            wc = wchunk()
                            bk = mk.bank()
                            mk.mm(bk[:], [(wc[:, kc, :], xm(kc)) for kc in range(KC)], reads=[wc, xt], writes=[bk])
                            bh = mk.bank()
                            mk.mm(bh[:, 0:HALO], [(wc[:, kc, :], xt[:, kc, 0:HALO]) for kc in range(KC)], reads=[wc, xt], writes=[bh], sig=False)
                            mk.mm(bh[:, HALO:2 * HALO], [(wc[:, kc, :], xt[:, kc, HALO + TT:XW]) for kc in range(KC)], reads=[wc, xt], writes=[bh])
                            Ug = U[g]
                            mk.op(act, lambda: A.copy(Ug[:, HALO:HALO + TT], bk[:]), reads=[bk], writes=[Ug])
                            if first_in_unit and unit == 1:
                                mk.op(dve, lambda: V.tensor_scalar_mul(Ug[:, 0:HALO], bh[:, 0:HALO], flags[:, 0:1]), reads=[bh, flags], writes=[Ug])
                            elif first_in_unit:
                                mk.op(dve, lambda: V.memset(Ug[:, 0:HALO], 0.0), writes=[Ug])
                            else:
                                mk.op(dve, lambda: V.tensor_copy(Ug[:, 0:HALO], bh[:, 0:HALO]), reads=[bh], writes=[Ug])
                            if last_in_unit and unit == 0:
                                mk.op(dve, lambda: V.tensor_scalar_mul(Ug[:, HALO + TT:XW], bh[:, HALO:2 * HALO], flags[:, 0:1]), reads=[bh, flags], writes=[Ug])
                            elif last_in_unit:
                                mk.op(dve, lambda: V.memset(Ug[:, HALO + TT:XW], 0.0), reads=[bh], writes=[Ug])
                            else:
                                mk.op(dve, lambda: V.tensor_copy(Ug[:, HALO + TT:XW], bh[:, HALO:2 * HALO]), reads=[bh], writes=[Ug])
                        for g in range(4):
                            Ug = U[g]
                            mk.op(dve, lambda: V.tensor_tensor(w2[:, 1:XW], Ug[:, 0:XW - 1], Ug[:, 1:XW], ALU.add), reads=[Ug], writes=[w2])
                            wsum = w2
                            if g >= 1:
                                mk.op(dve, lambda: V.tensor_tensor(w4[:, 2:XW - 1], w2[:, 1:XW - 2], w2[:, 3:XW], ALU.add), reads=[w2], writes=[w4])
                                wsum = w4
                            if g >= 2:
                                mk.op(dve, lambda: V.tensor_tensor(w8[:, 4:XW - 3], w4[:, 2:XW - 5], w4[:, 6:XW - 1], ALU.add), reads=[w4], writes=[w8])
                                wsum = w8
                            if g >= 3:
                                mk.op(dve, lambda: V.tensor_tensor(w16[:, 8:XW - 7], w8[:, 4:XW - 11], w8[:, 12:XW - 3], ALU.add), reads=[w8], writes=[w16])
                                wsum = w16
                            mk.op(pool, lambda: G.tensor_tensor(wsum[:, HALO:HALO + TT], wsum[:, HALO:HALO + TT], ic[:, g, :], ALU.mult), reads=[wsum, ic], writes=[wsum])
                            d_ = dT[g % 2]
                            mk.op(pool, lambda: G.tensor_tensor(d_[:], wsum[:, HALO:HALO + TT], Ug[:, HALO:HALO + TT], ALU.subtract), reads=[wsum, Ug], writes=[d_])
                            wc = wchunk()
                            bz = mk.bank()
                            mk.mm(bz[:], [(wc[:, kc, :], xm(kc)) for kc in range(KC)], reads=[wc, xt], writes=[bz])
                            sl_ = sil[si % NSIL]
                            si += 1
                            mk.op(act, lambda: A.activation(sl_[:], bz[:], AF.Silu), reads=[bz], writes=[sl_])
                            bp = mk.bank()
                            mk.mm(bp[:], [(poolw[:, g, :], d_[:])], reads=[poolw, d_], writes=[bp])
                            mk.op(dve, lambda: V.scalar_tensor_tensor(yain[:, g, :], bp[:], pscale[:, l, g:g + 1], sl_[:], ALU.mult, ALU.mult),
                                  reads=[bp, pscale, sl_], writes=[yain])
                        for h in range(8):
                            wc = wchunk()
                            bz = mk.bank()
                            mk.mm(bz[:], [(wc[:, kc, :], xm(kc)) for kc in range(KC)], reads=[wc, xt], writes=[bz])
                            sl_ = sil[si % NSIL]
                            si += 1
                            mk.op(act, lambda: A.activation(sl_[:], bz[:], AF.Silu), reads=[bz], writes=[sl_])
                            mk.op(pool, lambda: G.tensor_tensor(ybt[:, h, :], ybt[:, h, :], sl_[:], ALU.mult), reads=[ybt, sl_], writes=[ybt])
                        for h in range(4):
                            wc = wchunk()
                            bz = mk.bank()
                            mk.mm(bz[:], [(wc[:, kc, :], xm(kc)) for kc in range(KC)], reads=[wc, xt], writes=[bz])
                            mk.op(act, lambda: A.copy(mrg[:, h, :], bz[:]), reads=[bz], writes=[mrg])
                        for h in range(4):
                            wc = wchunk()
                            bz = mk.bank()
                            mk.mm(bz[:], [(wc[:, kc, :], xm(kc)) for kc in range(KC)], reads=[wc, xt], writes=[bz])
                            sl_ = sil[si % NSIL]
                            si += 1
                            mk.op(act, lambda: A.activation(sl_[:], bz[:], AF.Silu), reads=[bz], writes=[sl_])
                            bo = mk.bank()
                            bl = mk.bank()
                            for mb in range(2):
                                bs_ = mk.bank()
                                mk.mm(bs_[:], [(kmT[:, unit, h, mb * 128:(mb + 1) * 128], mrg[:, h, :])], reads=[kmT, mrg], writes=[bs_])
                                pm_ = pm[mb]
                                mk.op(act, lambda: A.activation(pm_[:], bs_[:], AF.Exp, scale=float(128 ** -0.5)), reads=[bs_], writes=[pm_])
                                mk.mm(bo[:], [(vm[:, unit, mb, h * 128:(h + 1) * 128], pm_[:])], reads=[vm, pm_], writes=[bo], start=(mb == 0), stop=(mb == 1))
                                mk.mm(bl[:], [(ones[:], pm_[:])], reads=[ones, pm_], writes=[bl], start=(mb == 0), stop=(mb == 1))
                            mk.op(dve, lambda: V.reciprocal(rl3[:], bl[:]), reads=[bl], writes=[rl3])
                            mk.op(dve, lambda: V.tensor_tensor(rl3[:], rl3[:], sl_[:], ALU.mult), reads=[rl3, sl_], writes=[rl3])
                            mk.op(dve, lambda: V.tensor_tensor(ycin[:, h, :], bo[:], rl3[:], ALU.mult), reads=[bo, rl3], writes=[ycin])
                        for dc in range(8):
                            cs_ = slice(dc * 128, (dc + 1) * 128)
                            gts = []
                            for jg in range(3):
                                wc = wchunk()
                                bg = mk.bank()
                                mk.mm(bg[:], [(wc[:, kc, :], xm(kc)) for kc in range(KC)], reads=[wc, xt], writes=[bg])
                                g_ = gt[gi % NGT]
                                gi += 1
                                mk.op(act, lambda: A.activation(g_[:], bg[:], AF.Sigmoid), reads=[bg], writes=[g_])
                                gts.append(g_)
                            ba = mk.bank()
                            mk.mm(ba[:], [(wbra[:, g, cs_], yain[:, g, :]) for g in range(4)], reads=[wbra, yain], writes=[ba])
                            mk.op(dve, lambda: V.tensor_tensor(ma[0][:], ba[:], gts[0][:], ALU.mult), reads=[ba, gts[0]], writes=[ma[0]])
                            bb = mk.bank()
                            mk.mm(bb[:], [(wbrb[:, h, cs_], ybt[:, h, :]) for h in range(8)], reads=[wbrb, ybt], writes=[bb])
                            mk.op(dve, lambda: V.tensor_tensor(ma[1][:], bb[:], gts[1][:], ALU.mult), reads=[bb, gts[1]], writes=[ma[1]])
                            bc = mk.bank()
                            mk.mm(bc[:], [(wbrc[:, h, cs_], ycin[:, h, :]) for h in range(4)], reads=[wbrc, ycin], writes=[bc])
                            mk.op(dve, lambda: V.tensor_tensor(ma[2][:], bc[:], gts[2][:], ALU.mult), reads=[bc, gts[2]], writes=[ma[2]])
                            mk.op(pool, lambda: G.tensor_tensor(ma[0][:], ma[0][:], ma[1][:], ALU.add), reads=[ma[0], ma[1]], writes=[ma[0]])
                            mk.op(pool, lambda: G.tensor_tensor(mrg[:, dc, :], ma[0][:], ma[2][:], ALU.add), reads=[ma[0], ma[2]], writes=[mrg])
                        for b in range(4):
                            xr = xres[0]
                            z = zb[b % 2]
                            mk.dma(sp, xrs[0], xr[:], xs[cur][ti * TT + b * 128: ti * TT + (b + 1) * 128, :], writes=[xr])
                            for half in range(2):
                                bo = mk.bank()
                                mk.mm(bo[:], [(mrg[:, dc, b * 128:(b + 1) * 128], wout[:, dc, half * 512:(half + 1) * 512]) for dc in range(8)],
                                      reads=[mrg, wout], writes=[bo])
                                mk.op(dve, lambda: V.scalar_tensor_tensor(z[:, half * 512:(half + 1) * 512], xr[:, half * 512:(half + 1) * 512],
                                                                          float((2.0 * 4) ** 0.25), bo[:], ALU.mult, ALU.add),
                                      reads=[xr, bo], writes=[z])
                            ln_block(lnb, ti * TT + b * 128, b, z, zss[b % 2], dst_xs, dst_xT is not None, gtab, btab)
                        if dst_xT is not None:
                            ln_flush(lnb, ti, dst_xT)
                    mk.barrier()
                    recycle()
        except _Stop:
            pass
        mk.dead = False
        mk.barrier()
        recycle()
    return nc


_CACHE = {}


def _core_assign(c, x_prompt, x_sample, mem_prompt, mem_sample):
    if c < 4:
        x = np.concatenate([x_prompt[c], x_sample[c]], axis=0)
        m = np.concatenate([mem_prompt[c], mem_prompt[c], mem_sample[c]], axis=0)
    else:
        s0 = 4 + 3 * (c - 4)
        x = np.concatenate([x_sample[s0], x_sample[s0 + 1], x_sample[s0 + 2]], axis=0)
        m = np.concatenate([mem_sample[s0], mem_sample[s0 + 1], mem_sample[s0 + 2]], axis=0)
    return np.ascontiguousarray(x), np.ascontiguousarray(m)


def kernel(x_prompt, x_sample, mem_prompt, mem_sample, ln_in_g, ln_in_b, w_in, w_mem_kv,
           pool_w, pool_scale, lam_q1, lam_k1, lam_q2, lam_k2, subln_g, w_br_a, w_br_b,
           w_br_c, w_out, ln_g, ln_b):
    f = lambda a: np.ascontiguousarray(np.asarray(a, dtype=np.float32))
    x_prompt, x_sample, mem_prompt, mem_sample = f(x_prompt), f(x_sample), f(mem_prompt), f(mem_sample)
    UL = x_sample.shape[1]
    DEPTH = w_in.shape[0]
    assert x_prompt.shape[1] == 2 * UL and x_prompt.shape[0] == 4 and x_sample.shape[0] == 16
    key = (UL, DEPTH)
    if key not in _CACHE:
        _CACHE[key] = build(UL, DEPTH)
    nc = _CACHE[key]
    rT, ident, ones, onesm = _const_tables()
    shared = {"ln_in_g": f(ln_in_g), "ln_in_b": f(ln_in_b), "w_in": f(w_in), "w_mem_kv": f(w_mem_kv),
              "pool_w": f(pool_w), "pool_scale": f(pool_scale), "lam_q1": f(lam_q1), "lam_k1": f(lam_k1),
              "lam_q2": f(lam_q2), "lam_k2": f(lam_k2), "subln_g": f(subln_g), "w_br_a": f(w_br_a),
              "w_br_b": f(w_br_b), "w_br_c": f(w_br_c), "w_out": f(w_out), "ln_g": f(ln_g), "ln_b": f(ln_b),
              "rT": rT, "ident": ident, "ones": ones, "onesm": onesm}
    in_maps = []
    for c in range(NCORES):
        x, m = _core_assign(c, x_prompt, x_sample, mem_prompt, mem_sample)
        ropeC, ropeS, invc, flags = _host_tables(UL, c)
        d = dict(shared)
        d.update({"x": x, "mem": m, "ropeC": ropeC, "ropeS": ropeS, "invc": invc, "flags": flags})
        in_maps.append(d)
    res = run_bass_kernel_spmd(nc, in_maps, core_ids=list(range(NCORES)))
    y_prompt = np.empty_like(x_prompt)
    y_sample = np.empty_like(x_sample)
    for c in range(NCORES):
        y = np.asarray(res.results[c]["y"], dtype=np.float32)
        if c < 4:
            y_prompt[c] = y[:2 * UL]
            y_sample[c] = y[2 * UL:]
        else:
            s0 = 4 + 3 * (c - 4)
            for k in range(3):
                y_sample[s0 + k] = y[k * UL:(k + 1) * UL]
    return (y_prompt, y_sample)
```

```python
import math
import contextlib
import numpy as np
import ml_dtypes
import concourse.bass as bass
import concourse.mybir as mybir
from concourse.bass_utils import run_bass_kernel_spmd

F32 = mybir.dt.float32
BF16 = mybir.dt.bfloat16
AF = mybir.ActivationFunctionType
ALU = mybir.AluOpType

NCORES = 8
D = 1024
KC = 8
TT = 512
HALO = 8
NEG = -30000.0
LN_EPS = 1e-5
SUBLN_EPS = 1e-5
ROPE_THETA = 500000.0
SEM_CAP = 60000


class _Stop(Exception):
    pass


class Eng:
    def __init__(self, mk, raw, name, selfsync):
        self.mk, self.raw, self.name, self.selfsync = mk, raw, name, selfsync
        self.sem = None
        self.semid = None
        self.cnt = 0
        self.seen = {}
        self.last = None

    def wait(self, toks):
        best = {}
        for t in toks:
            if t is None:
                continue
            semid, val, owner = t
            if owner is self and not self.selfsync:
                continue
            if self.seen.get(semid, 0) >= val:
                continue
            if best.get(semid, 0) < val:
                best[semid] = val
        for semid, val in best.items():
            self.raw.wait_ge(self.mk.sems[semid], val)
            self.seen[semid] = val

    def sig(self, ins):
        if self.sem is None or self.cnt >= SEM_CAP:
            self.semid, self.sem = self.mk.newsem(self.name)
            self.cnt = 0
        self.cnt += 1
        ins.then_inc(self.sem, 1)
        self.last = (self.semid, self.cnt, self)
        return self.last


class Slot:
    def __init__(self, mk, name):
        self.mk, self.name = mk, name
        self.sem = None
        self.semid = None
        self.cnt = 0

    def inc(self, ins):
        if self.sem is None or self.cnt >= SEM_CAP:
            self.semid, self.sem = self.mk.newsem(self.name)
            self.cnt = 0
        self.cnt += 16
        ins.then_inc(self.sem, 16)
        tok = (self.semid, self.cnt, None)
        self.mk.pending[self.semid] = tok
        return tok


class Buf:
    def __init__(self, ap_src, name="", psum=False):
        self.t = ap_src
        self.name = name
        self.psum = psum
        self.wr = []
        self.rd = {}

    def __getitem__(self, k):
        return self.t[k]


class MK:
    def __init__(self, nc):
        self.nc = nc
        self.sems = []
        self.pending = {}
        self.pe = Eng(self, nc.tensor, "pe", False)
        self.act = Eng(self, nc.scalar, "act", True)
        self.dve = Eng(self, nc.vector, "dve", True)
        self.pool = Eng(self, nc.gpsimd, "pool", True)
        self.sp = Eng(self, nc.sync, "sp", True)
        self.engs = [self.pe, self.act, self.dve, self.pool, self.sp]
        self.uid = 0
        self.dead = False
        self.banks = []
        self.bi = 0

    def newsem(self, name):
        self.uid += 1
        s = self.nc.alloc_semaphore(name=f"{name}_{self.uid}")
        self.sems.append(s)
        return len(self.sems) - 1, s

    def _deps(self, reads, writes, extra, eng=None):
        w = list(extra)
        for b in reads:
            w += b.wr
            if b.psum:
                w += [t for t in b.rd.values() if t[2] is not eng]
        for b in writes:
            w += b.wr
            w += list(b.rd.values())
        return w

    def _commit(self, tok, reads, writes):
        key = tok[0]
        for b in reads:
            b.rd[key] = tok
        for b in writes:
            b.wr = [tok]
            b.rd = {}

    def op(self, eng, fn, reads=(), writes=(), extra=()):
        if self.dead:
            return None
        eng.wait(self._deps(reads, writes, extra, eng))
        ins = fn()
        tok = eng.sig(ins)
        self._commit(tok, reads, writes)
        return tok

    def mm(self, out_ap, ops, reads=(), writes=(), extra=(), start=True, stop=True, sig=True):
        pe = self.pe
        if self.dead:
            return None
        pe.wait(self._deps(reads, writes, extra))
        n = len(ops)
        ins = None
        for i, (l, r) in enumerate(ops):
            ins = self.nc.tensor.matmul(out_ap, l, r, start=(start and i == 0), stop=(stop and i == n - 1))
        if sig:
            tok = pe.sig(ins)
            self._commit(tok, reads, writes)
            return tok
        return None

    def dma(self, q, slot, out, in_, reads=(), writes=(), extra=()):
        if self.dead:
            return None
        q.wait(self._deps(reads, writes, extra))
        ins = q.raw.dma_start(out=out, in_=in_)
        tok = slot.inc(ins)
        self._commit(tok, reads, writes)
        return tok

    def dma_group(self, q, slot, items, bufs):
        if self.dead:
            return None
        q.wait(self._deps([], bufs, []))
        for out, in_ in items:
            slot.inc(q.raw.dma_start(out=out, in_=in_))
        tok = (slot.semid, slot.cnt, None)
        self._commit(tok, [], bufs)
        return tok

    def finalize(self, slot, bufs):
        tok = (slot.semid, slot.cnt, None)
        for b in bufs:
            b.wr = [tok]

    def bank(self):
        b = self.banks[self.bi % len(self.banks)]
        self.bi += 1
        return b

    def barrier(self):
        if self.dead:
            return
        toks = [e.last for e in self.engs if e.last is not None] + list(self.pending.values())
        for e in self.engs:
            e.wait(toks)


def _host_tables(UL, core):
    T = 3 * UL
    split = core >= 4
    pos = np.zeros(T, np.float32)
    for u in range(3):
        base = 0.0
        if u == 1 and not split:
            base = float(UL)
        pos[u * UL:(u + 1) * UL] = np.arange(UL, dtype=np.float32) + np.float32(base)
    inv = (np.float32(ROPE_THETA) ** (-np.arange(0, 16, 2, dtype=np.float32) / np.float32(16))).astype(np.float32)
    ang = (pos[:, None] * inv[None, :]).astype(np.float32)
    cos = np.cos(ang).astype(np.float32)
    sin = np.sin(ang).astype(np.float32)
    ropeC = np.ones((128, T), np.float32)
    ropeS = np.zeros((128, T), np.float32)
    for b in (0, 64):
        for m in range(8):
            ropeC[b + m] = cos[:, m]
            ropeC[b + 8 + m] = cos[:, m]
            ropeS[b + m] = sin[:, m]
            ropeS[b + 8 + m] = sin[:, m]
    seq_start = np.zeros(T, np.int64)
    seq_end = np.zeros(T, np.int64)
    for u in range(3):
        s, e = u * UL, (u + 1) * UL
        if u < 2 and not split:
            s, e = 0, 2 * UL
        seq_start[u * UL:(u + 1) * UL] = s
        seq_end[u * UL:(u + 1) * UL] = e
    t = np.arange(T)
    invc = np.zeros((4, T), np.float32)
    for g, w in enumerate((2, 4, 8, 16)):
        lo = np.maximum(t - w // 2, seq_start)
        hi = np.minimum(t + w // 2 - 1, seq_end - 1)
        invc[g] = (1.0 / (hi - lo + 1)).astype(np.float32)
    flags = np.zeros((128, 2), np.float32)
    flags[:, 0] = 0.0 if split else 1.0
    flags[:, 1] = NEG if split else 0.0
    return ropeC, ropeS, invc, flags


def _const_tables():
    R = np.zeros((128, 128), np.float32)
    for b in (0, 64):
        for m in range(8):
            R[b + m, b + m + 8] = -1.0
            R[b + 8 + m, b + m] = 1.0
    rT = np.ascontiguousarray(R.T).astype(ml_dtypes.bfloat16)
    ident = np.eye(128, dtype=np.float32)
    ones = np.ones((128, 128), np.float32).astype(ml_dtypes.bfloat16)
    onesm = (np.ones((128, 128), np.float32) / 128.0).astype(ml_dtypes.bfloat16)
    return rT, ident, ones, onesm


def build(UL, DEPTH):
    import os
    STOP = float(os.environ.get("MK_STOP", "99"))
    T = 3 * UL
    NT = T // TT
    TPU = UL // TT
    nc = bass.Bass("TRN2", target_bir_lowering=False)
    mk = MK(nc)
    pe, act, dve, pool, sp = mk.pe, mk.act, mk.dve, mk.pool, mk.sp

    def din(name, shape, dt=F32):
        return nc.dram_tensor(name, list(shape), dt, kind="ExternalInput").ap()

    def dscr(name, shape, dt):
        return nc.dram_tensor(name, list(shape), dt, kind="Internal").ap()

    x_in = din("x", [T, D])
    mem_in = din("mem", [3 * 256, D])
    ln_in_g = din("ln_in_g", [D])
    ln_in_b = din("ln_in_b", [D])
    w_in = din("w_in", [DEPTH, D, 9216])
    w_mem = din("w_mem_kv", [DEPTH, D, 1024])
    pool_w = din("pool_w", [DEPTH, 4, 128, 128])
    pool_scale = din("pool_scale", [DEPTH, 512])
    lamv = [din(n, [DEPTH, 64]) for n in ("lam_q1", "lam_k1", "lam_q2", "lam_k2")]
    subln_g = din("subln_g", [DEPTH, 128])
    w_br_a = din("w_br_a", [DEPTH, 512, D])
    w_br_b = din("w_br_b", [DEPTH, 1024, D])
    w_br_c = din("w_br_c", [DEPTH, 512, D])
    w_out = din("w_out", [DEPTH, D, D])
    ln_g = din("ln_g", [DEPTH, D])
    ln_b = din("ln_b", [DEPTH, D])
    ropeC_d = din("ropeC", [128, T])
    ropeS_d = din("ropeS", [128, T])
    invc_d = din("invc", [4, T])
    flags_d = din("flags", [128, 2])
    rT_d = din("rT", [128, 128], BF16)
    ident_d = din("ident", [128, 128])
    ones_d = din("ones", [128, 128], BF16)
    onesm_d = din("onesm", [128, 128], BF16)
    y_out = nc.dram_tensor("y", [T, D], F32, kind="ExternalOutput").ap()

    win_b = dscr("win_b", [DEPTH, 72, 128, KC, 128], BF16)
    wv_b = dscr("wv_b", [DEPTH, 128, KC, 1024], BF16)
    wmem_b = dscr("wmem_b", [DEPTH, 128, KC, 1024], BF16)
    poolw_b = dscr("poolw_b", [DEPTH, 128, 4, 128], BF16)
    wbra_b = dscr("wbra_b", [DEPTH, 128, 4, 1024], BF16)
    wbrb_b = dscr("wbrb_b", [DEPTH, 128, 8, 1024], BF16)
    wbrc_b = dscr("wbrc_b", [DEPTH, 128, 4, 1024], BF16)
    wout_b = dscr("wout_b", [DEPTH, 128, 8, 1024], BF16)
    xs = [dscr(f"xs{i}", [T, D], F32) for i in range(2)]
    xT = [dscr(f"xT{i}", [D, T], BF16) for i in range(2)]
    qT = dscr("qT", [D, T], BF16)
    kT = dscr("kT", [D, T], BF16)
    vv = dscr("vv", [T, D], BF16)
    ybT = dscr("ybT", [D, T], BF16)
    memT = dscr("memT", [D, 3 * 256], BF16)

    es = contextlib.ExitStack()
    cnt = [0]

    def sb(ctx, shape, dt, name="t"):
        cnt[0] += 1
        return Buf(ctx.enter_context(nc.sbuf_tensor(f"{name}{cnt[0]}", list(shape), dt)), name)

    free_slots = []
    cur_slots = []

    def slot(name):
        s_ = free_slots.pop() if free_slots else Slot(mk, name)
        cur_slots.append(s_)
        return s_

    def recycle():
        free_slots.extend(cur_slots)
        cur_slots.clear()

    def stage(n):
        if n > STOP:
            mk.dead = True

    V = nc.vector
    A = nc.scalar
    G = nc.gpsimd

    with es:
        for i in range(8):
            mk.banks.append(Buf(es.enter_context(nc.psum_tensor(f"ps{i}", [128, 512], F32)), f"ps{i}", psum=True))
        rT = sb(es, [128, 128], BF16, "rT")
        ident = sb(es, [128, 128], F32, "ident")
        ones = sb(es, [128, 128], BF16, "ones")
        onesm = sb(es, [128, 128], BF16, "onesm")
        flags = sb(es, [128, 2], F32, "flags")
        zcol = sb(es, [128, 1], F32, "zcol")
        neglam = sb(es, [128, DEPTH], F32, "neglam")
        gsub = sb(es, [128, DEPTH], F32, "gsub")
        pscale = sb(es, [128, DEPTH, 4], F32, "pscale")
        kmT = sb(es, [128, 3, 4, 256], BF16, "kmT")
        vm = sb(es, [128, 3, 2, 512], BF16, "vm")
        cs = slot("const")
        items = [(b[:], d_) for (b, d_) in ((rT, rT_d), (ident, ident_d), (ones, ones_d), (onesm, onesm_d), (flags, flags_d))]
        for l in range(DEPTH):
            items.append((gsub[:, l:l + 1], subln_g[l].rearrange("(p o) -> p o", o=1)))
            for g in range(4):
                items.append((pscale[:, l, g:g + 1], pool_scale[l, g * 128:(g + 1) * 128].rearrange("(p o) -> p o", o=1)))
        mk.dma_group(sp, cs, items, [rT, ident, ones, onesm, flags, gsub, pscale])
        mk.op(dve, lambda: V.memset(zcol[:], 0.0), writes=[zcol])
        ones32 = sb(es, [128, 128], F32, "ones32")
        mk.op(dve, lambda: V.memset(ones32[:], 1.0), writes=[ones32])
        epscol = sb(es, [128, 1], F32, "epscol")
        mk.op(dve, lambda: V.memset(epscol[:], LN_EPS), writes=[epscol])

        try:
            wslot = Slot(mk, "wcast")
            for l in range(DEPTH):
                for ch in range(72):
                    if 24 <= ch < 32:
                        continue
                    mk.dma(pool, wslot, win_b[l, ch],
                           w_in[l, :, ch * 128:(ch + 1) * 128].rearrange("(kc p) c -> p kc c", p=128))
                for kc in range(KC):
                    mk.dma(pool, wslot, wv_b[l, :, kc, :], w_in[l, kc * 128:(kc + 1) * 128, 3072:4096])
                    mk.dma(pool, wslot, wmem_b[l, :, kc, :], w_mem[l, kc * 128:(kc + 1) * 128, :])
                    mk.dma(pool, wslot, wbrb_b[l, :, kc, :], w_br_b[l, kc * 128:(kc + 1) * 128, :])
                    mk.dma(pool, wslot, wout_b[l, :, kc, :], w_out[l, kc * 128:(kc + 1) * 128, :])
                for g in range(4):
                    mk.dma(pool, wslot, wbra_b[l, :, g, :], w_br_a[l, g * 128:(g + 1) * 128, :])
                    mk.dma(pool, wslot, wbrc_b[l, :, g, :], w_br_c[l, g * 128:(g + 1) * 128, :])
                    mk.dma(pool, wslot, poolw_b[l, :, g, :], pool_w[l, g])

            stage(2)
            with contextlib.ExitStack() as ph:
                lt = [sb(ph, [128, DEPTH * 64], F32, "lam") for _ in range(4)]
                pr = sb(ph, [128, DEPTH * 64], F32, "lamp")
                s1 = sb(ph, [128, DEPTH], F32, "lams1")
                s2 = sb(ph, [128, DEPTH], F32, "lams2")
                cs2 = slot("const2")
                mk.dma_group(sp, cs2, [(lt[i][:], lamv[i].rearrange("l d -> (l d)").partition_broadcast(128)) for i in range(4)], lt)
                for (a_, b_, s_) in ((0, 1, s1), (2, 3, s2)):
                    mk.op(dve, lambda: V.tensor_tensor(pr[:], lt[a_][:], lt[b_][:], ALU.mult), reads=[lt[a_], lt[b_]], writes=[pr])
                    for l in range(DEPTH):
                        mk.op(dve, lambda: V.tensor_reduce(s_[:, l:l + 1], pr[:, l * 64:(l + 1) * 64], mybir.AxisListType.X, ALU.add), reads=[pr], writes=[s_])
                    mk.op(act, lambda: A.activation(s_[:], s_[:], AF.Exp), reads=[s_], writes=[s_])
                mk.op(dve, lambda: V.tensor_tensor(neglam[:], s2[:], s1[:], ALU.subtract), reads=[s1, s2], writes=[neglam])
                for l in range(DEPTH):
                    li = 0.8 - 0.6 * math.exp(-0.3 * l)
                    mk.op(dve, lambda: V.tensor_scalar_add(neglam[:, l:l + 1], neglam[:, l:l + 1], -li), reads=[neglam], writes=[neglam])
                    mk.op(dve, lambda: V.tensor_scalar_mul(gsub[:, l:l + 1], gsub[:, l:l + 1], 1.0 - li), reads=[gsub], writes=[gsub])
                mk.barrier()
                recycle()

            def ln_block(lnb, row0, b, z, zslot, dst_xs, want_T, gtab, btab):
                st, mv, rstd, xTo, sl_t = lnb
                mk.op(dve, lambda: V.bn_stats(st[:, 0:6], z[:, 0:512]), reads=[z], writes=[st])
                mk.op(dve, lambda: V.bn_stats(st[:, 6:12], z[:, 512:1024]), reads=[z], writes=[st])
                mk.op(dve, lambda: V.bn_aggr(mv[:, 0:2], st[:, 0:12]), reads=[st], writes=[mv])
                mk.op(act, lambda: A.activation(rstd[:], mv[:, 1:2], AF.Ln, bias=epscol[:]), reads=[mv, epscol], writes=[rstd])
                mk.op(act, lambda: A.activation(rstd[:], rstd[:], AF.Exp, scale=-0.5), reads=[rstd], writes=[rstd])
                mk.op(dve, lambda: V.tensor_scalar(z[:], z[:], mv[:, 0:1], rstd[:], ALU.subtract, ALU.mult), reads=[z, mv, rstd], writes=[z])
                mk.op(pool, lambda: G.tensor_tensor(z[:], z[:], gtab[:], ALU.mult), reads=[z, gtab], writes=[z])
                mk.op(pool, lambda: G.tensor_tensor(z[:], z[:], btab[:], ALU.add), reads=[z, btab], writes=[z])
                mk.dma(sp, zslot, dst_xs[row0: row0 + 128, :], z[:], reads=[z])
                if want_T:
                    for half in range(2):
                        if mk.dead:
                            break
                        bk = mk.bank()
                        pe.wait(mk._deps([z, ident], [bk], []))
                        ins = None
                        for f in range(4):
                            fc = half * 4 + f
                            ins = nc.tensor.transpose(bk[:, f * 128:(f + 1) * 128], z[:, fc * 128:(fc + 1) * 128], ident[:])
                        tk = pe.sig(ins)
                        mk._commit(tk, [z, ident], [bk])
                        for f in range(4):
                            fc = half * 4 + f
                            mk.op(act, lambda: A.copy(xTo[:, fc, b * 128:(b + 1) * 128], bk[:, f * 128:(f + 1) * 128]), reads=[bk], writes=[xTo])

            def ln_flush(lnb, tile_i, dst_xT):
                st, mv, rstd, xTo, sl_t = lnb
                mk.dma(sp, sl_t, dst_xT[:, tile_i * TT:(tile_i + 1) * TT].rearrange("(kc p) t -> p kc t", p=128), xTo[:], reads=[xTo])

            stage(3)
            with contextlib.ExitStack() as ph:
                gtab = sb(ph, [128, D], F32, "gtab")
                btab = sb(ph, [128, D], F32, "btab")
                cs3 = slot("const3")
                mk.dma_group(sp, cs3, [(gtab[:], ln_in_g.partition_broadcast(128)), (btab[:], ln_in_b.partition_broadcast(128))], [gtab, btab])
                lnb = (sb(ph, [128, 12], F32, "st"), sb(ph, [128, 2], F32, "mv"), sb(ph, [128, 1], F32, "rstd"),
                       sb(ph, [128, 8, TT], BF16, "xTo"), slot("p0t"))
                zr = [sb(ph, [128, D], F32, "z") for _ in range(4)]
                zls = [slot(f"p0l{i}") for i in range(4)]
                zss = [slot(f"p0s{i}") for i in range(4)]
                nblk = T // 128

                def p0_load(bi):
                    mk.dma(sp, zls[bi % 4], zr[bi % 4][:], x_in[bi * 128:(bi + 1) * 128, :], writes=[zr[bi % 4]])

                p0_load(0)
                p0_load(1)
                for bi in range(nblk):
                    if bi + 2 < nblk:
                        p0_load(bi + 2)
                    ln_block(lnb, bi * 128, bi % 4, zr[bi % 4], zss[bi % 4], xs[0], True, gtab, btab)
                    if bi % 4 == 3:
                        ln_flush(lnb, bi // 4, xT[0])
                zsets = [zr[0:2], zr[2:4]]
                zsl = [zls[0], zls[2]]
                mT = sb(ph, [128, 8, 256], BF16, "mT")
                msl = slot("p0m")
                for u in range(3):
                    zb = zsets[u % 2]
                    mk.dma_group(sp, zsl[u % 2], [(zb[mb][:], mem_in[u * 256 + mb * 128: u * 256 + (mb + 1) * 128, :]) for mb in range(2)], zb[0:2])
                    for fc in range(8):
                        if mk.dead:
                            break
                        bk = mk.bank()
                        pe.wait(mk._deps(zb[0:2], [bk], []))
                        for mb in range(2):
                            ins = nc.tensor.transpose(bk[:, mb * 128:(mb + 1) * 128], zb[mb][:, fc * 128:(fc + 1) * 128], ident[:])
                        tk = pe.sig(ins)
                        mk._commit(tk, zb[0:2], [bk])
                        mk.op(act, lambda: A.copy(mT[:, fc, :], bk[:, 0:256]), reads=[bk], writes=[mT])
                    mk.dma(sp, msl, memT[:, u * 256:(u + 1) * 256].rearrange("(kc p) m -> p kc m", p=128), mT[:], reads=[mT])
                mk.barrier()
                recycle()

            for l in range(DEPTH):
                cur, nxt = l % 2, (l + 1) % 2
                last = (l == DEPTH - 1)
                stage(4)
                with contextlib.ExitStack() as ph:
                    wqk = sb(ph, [128, 16, KC, 128], BF16, "wqk")
                    wv = sb(ph, [128, KC, 1024], BF16, "wv")
                    wm = sb(ph, [128, KC, 1024], BF16, "wm")
                    mTa = sb(ph, [128, KC, 768], BF16, "mTa")
                    ws = slot("p1w")
                    items = [(wqk[:, c8:c8 + 4], win_b[l, 8 + c8: 8 + c8 + 4].rearrange("ch p kc c -> p ch kc c")) for c8 in range(0, 16, 4)]
                    items += [(wv[:], wv_b[l]), (wm[:], wmem_b[l]), (mTa[:], memT.rearrange("(kc p) m -> p kc m", p=128))]
                    mk.dma_group(sp, ws, items, [wqk, wv, wm, mTa])
                    for u in range(3):
                        for h in range(4):
                            bk = mk.bank()
                            mk.mm(bk[:, 0:256], [(wm[:, kc, h * 128:(h + 1) * 128], mTa[:, kc, u * 256:(u + 1) * 256]) for kc in range(KC)],
                                  reads=[wm, mTa], writes=[bk])
                            mk.op(act, lambda: A.copy(kmT[:, u, h, :], bk[:, 0:256]), reads=[bk], writes=[kmT])
                        for mb in range(2):
                            bk = mk.bank()
                            mk.mm(bk[:], [(mTa[:, kc, u * 256 + mb * 128: u * 256 + (mb + 1) * 128], wm[:, kc, 512:1024]) for kc in range(KC)],
                                  reads=[wm, mTa], writes=[bk])
                            mk.op(dve, lambda: V.tensor_copy(vm[:, u, mb, :], bk[:]), reads=[bk], writes=[vm])
                    stage(4.1)
                    xts = [sb(ph, [128, KC, TT], BF16, "xt") for _ in range(2)]
                    rcs = [sb(ph, [128, TT], F32, "rc") for _ in range(2)]
                    rss = [sb(ph, [128, TT], F32, "rs") for _ in range(2)]
                    lsl = [slot("p1l0"), slot("p1l1")]
                    qraw = [sb(ph, [128, TT], BF16, "qraw") for _ in range(2)]
                    t1s = [sb(ph, [128, TT], F32, "t1") for _ in range(2)]
                    t2s = [sb(ph, [128, TT], F32, "t2") for _ in range(2)]
                    qo = [sb(ph, [128, TT], BF16, "qo") for _ in range(4)]
                    qsl = [slot(f"p1q{i}") for i in range(4)]
                    vo = [sb(ph, [128, 4, 1024], BF16, "vo") for _ in range(2)]
                    vsl = [slot("p1v0"), slot("p1v1")]

                    def p1_load(ti):
                        i = ti % 2
                        mk.dma_group(sp, lsl[i], [(xts[i][:], xT[cur][:, ti * TT:(ti + 1) * TT].rearrange("(kc p) t -> p kc t", p=128)),
                                                  (rcs[i][:], ropeC_d[:, ti * TT:(ti + 1) * TT]),
                                                  (rss[i][:], ropeS_d[:, ti * TT:(ti + 1) * TT])], [xts[i], rcs[i], rss[i]])

                    p1_load(0)
                    qi_box = [0]
                    for ti in range(NT):
                        if ti + 1 < NT:
                            p1_load(ti + 1)
                        i = ti % 2
                        xt, rc, rs = xts[i], rcs[i], rss[i]
                        items16 = [(which, h) for which in (0, 1) for h in range(8)]

                        def p1_proj(k):
                            nonlocal_qi = qi_box[0]
                            qi_box[0] += 1
                            which, h = items16[k]
                            bk = mk.bank()
                            mk.mm(bk[:], [(wqk[:, which * 8 + h, kc, :], xt[:, kc, :]) for kc in range(KC)], reads=[wqk, xt], writes=[bk])
                            qr = qraw[nonlocal_qi % 2]
                            mk.op(act, lambda: A.copy(qr[:], bk[:]), reads=[bk], writes=[qr])
                            return bk, qr, nonlocal_qi

                        cur_p = p1_proj(0)
                        for k in range(16):
                            nxt_p = p1_proj(k + 1) if k + 1 < 16 else None
                            which, h = items16[k]
                            dst = qT if which == 0 else kT
                            bk, qr, qi_ = cur_p
                            t1, t2, q_o, q_s = t1s[qi_ % 2], t2s[qi_ % 2], qo[qi_ % 4], qsl[qi_ % 4]
                            bk2 = mk.bank()
                            mk.mm(bk2[:], [(rT[:], qr[:])], reads=[rT, qr], writes=[bk2])
                            mk.op(dve, lambda: V.tensor_tensor(t1[:], bk[:], rc[:], ALU.mult), reads=[bk, rc], writes=[t1])
                            mk.op(dve, lambda: V.tensor_tensor(t2[:], bk2[:], rs[:], ALU.mult), reads=[bk2, rs], writes=[t2])
                            mk.op(pool, lambda: G.tensor_tensor(q_o[:], t1[:], t2[:], ALU.add), reads=[t1, t2], writes=[q_o])
                            mk.dma(sp, q_s, dst[h * 128:(h + 1) * 128, ti * TT:(ti + 1) * TT], q_o[:], reads=[q_o])
                            cur_p = nxt_p
                        stage(4.2)
                        v_o = vo[i]
                        for b in range(4):
                            for half in range(2):
                                bk = mk.bank()
                                mk.mm(bk[:], [(xt[:, kc, b * 128:(b + 1) * 128], wv[:, kc, half * 512:(half + 1) * 512]) for kc in range(KC)],
                                      reads=[wv, xt], writes=[bk])
                                if half == 0:
                                    mk.op(act, lambda: A.copy(v_o[:, b, 0:512], bk[:]), reads=[bk], writes=[v_o])
                                else:
                                    mk.op(dve, lambda: V.tensor_copy(v_o[:, b, 512:1024], bk[:]), reads=[bk], writes=[v_o])
                        mk.dma(sp, vsl[i], vv[ti * TT:(ti + 1) * TT, :].rearrange("(b p) n -> p b n", p=128), v_o[:], reads=[v_o])
                    mk.barrier()
                    recycle()

                stage(5)
                with contextlib.ExitStack() as ph:
                    KL = 2 * UL
                    kts = [sb(ph, [128, KL], BF16, "kt") for _ in range(2)]
                    vts = [sb(ph, [128, KL // 128, 128], BF16, "vt") for _ in range(2)]
                    kvs = [slot("p2kv0"), slot("p2kv1")]
                    qts = [[sb(ph, [128, TT], BF16, "qta"), sb(ph, [128, TT], BF16, "qtb")] for _ in range(3)]
                    for qq in qts:
                        for qb_ in qq:
                            mk.op(pool, lambda: G.memset(qb_[:], 0.0), writes=[qb_])
                    qss = [slot(f"p2q{i}") for i in range(3)]
                    NPT = 6
                    pts = [sb(ph, [128, TT], BF16, "pt") for _ in range(NPT)]
                    accs = [sb(ph, [128, TT], F32, "acc") for _ in range(4)]
                    rL = sb(ph, [128, TT], F32, "rL")
                    oc = [sb(ph, [128, TT], F32, "oc") for _ in range(2)]
                    od = sb(ph, [128, TT], F32, "od")
                    osq = sb(ph, [128, TT], BF16, "osq")
                    rst = sb(ph, [128, TT], F32, "rst")
                    ybo = [sb(ph, [128, TT], BF16, "ybo") for _ in range(2)]
                    yss = [slot("p2y0"), slot("p2y1")]
                    Sb = [mk.banks[0], mk.banks[1], mk.banks[7]]
                    Ob = [mk.banks[2], mk.banks[3]]
                    Lb = [mk.banks[4], mk.banks[5]]
                    Mb = mk.banks[6]
                    slots_ = [(0, 2 * UL, list(range(0, 2 * TPU))), (2 * UL, 3 * UL, list(range(2 * TPU, 3 * TPU)))]
                    sh = [(s_, h) for s_ in range(2) for h in range(8)]

                    def p2_loadkv(idx):
                        s_, h = sh[idx]
                        k0, k1, _ = slots_[s_]
                        i = idx % 2
                        n = k1 - k0
                        items = [(kts[i][:, 0:n], kT[h * 128:(h + 1) * 128, k0:k1])]
                        for c0 in range(0, n // 128, 8):
                            c1 = min(c0 + 8, n // 128)
                            items.append((vts[i][:, c0:c1, :],
                                          vv[k0 + c0 * 128:k0 + c1 * 128, h * 128:(h + 1) * 128].rearrange("(c p) e -> p c e", p=128)))
                        mk.dma_group(sp, kvs[i], items, [kts[i], vts[i]])

                    qjobs = [(idx, qt) for idx in range(len(sh)) for qt in slots_[sh[idx][0]][2]]

                    def p2_loadq(j):
                        idx, qt = qjobs[j]
                        h = sh[idx][1]
                        qa_, qb_ = qts[j % 3]
                        mk.dma_group(sp, qss[j % 3], [(qa_[0:64, :], qT[h * 128:h * 128 + 64, qt * TT:(qt + 1) * TT]),
                                                      (qb_[64:128, :], qT[h * 128 + 64:(h + 1) * 128, qt * TT:(qt + 1) * TT])], [qa_, qb_])

                    p2_loadkv(0)
                    p2_loadq(0)
                    p2_loadq(1)
                    gidx = 0
                    jn = 0
                    deferred = []
                    loaded_kv = 0
                    for j, (idx, qt) in enumerate(qjobs):
                        s_, h = sh[idx]
                        k0, k1, qlist = slots_[s_]
                        if qt == qlist[0] and idx + 1 < len(sh) and loaded_kv < idx + 1:
                            p2_loadkv(idx + 1)
                            loaded_kv = idx + 1
                        if j + 2 < len(qjobs):
                            p2_loadq(j + 2)
                        kt, vt, qtile = kts[idx % 2], vts[idx % 2], qts[j % 3]
                        nch = (k1 - k0) // 128
                        qunit = qt // TPU
                        for c in range(2):
                            O, L = Ob[gidx % 2], Lb[gidx % 2]
                            gidx += 1
                            r0 = c * 64

                            qpad = qtile[c]

                            def qk(jc):
                                S = Sb[(jn + jc) % 3]
                                mk.mm(S[:], [(kt[:, jc * 128:(jc + 1) * 128], qpad[:])], reads=[kt, qpad], writes=[S])

                            qk(0)
                            if nch > 1:
                                qk(1)
                            for jc in range(nch):
                                S = Sb[(jn + jc) % 3]
                                P = pts[(jn + jc) % NPT]
                                kunit = (k0 + jc * 128) // UL
                                bias = flags[:, 1:2] if (kunit != qunit) else zcol[:]
                                mk.op(act, lambda: A.activation(P[:], S[:], AF.Exp, bias=bias, scale=0.125), reads=[S, flags, zcol], writes=[P])
                                if jc + 2 < nch:
                                    qk(jc + 2)
                                lastc = (jc == nch - 1)
                                mk.mm(O[:], [(vt[:, jc, :], P[:])], reads=[vt, P], writes=[O], start=(jc == 0), stop=lastc, sig=True)
                                ab_ = ((gidx - 1) % 2) * 2
                                if jc % 4 == 3:
                                    mk.mm(L[:], [(ones[:], P[:])], reads=[ones, P], writes=[L],
                                          start=(jc // 4 == 0), stop=(jc // 4 == nch // 4 - 1), sig=True)
                                else:
                                    dj = jc - jc // 4
                                    acc_ = accs[ab_ + (dj % 2)]
                                    if dj < 2:
                                        mk.op(dve, lambda: V.tensor_copy(acc_[:], P[:]), reads=[P], writes=[acc_])
                                    else:
                                        mk.op(dve, lambda: V.tensor_tensor(acc_[:], acc_[:], P[:], ALU.add), reads=[P, acc_], writes=[acc_])
                                while deferred and deferred[0][0] <= 0:
                                    deferred.pop(0)[1]()
                                for dfe in deferred:
                                    dfe[0] -= 1
                            jn += nch
                            ocb = oc[c]
                            aA, aB = accs[((gidx - 1) % 2) * 2], accs[((gidx - 1) % 2) * 2 + 1]
                            mk.op(dve, lambda: V.tensor_tensor(aA[:], aA[:], aB[:], ALU.add), reads=[aA, aB], writes=[aA])
                            mk.mm(Mb[:], [(ones32[:], aA[:])], reads=[ones32, aA], writes=[Mb])
                            mk.op(dve, lambda: V.tensor_copy(rL[:], L[:]), reads=[L], writes=[rL])
                            mk.op(dve, lambda: V.tensor_tensor(rL[:], rL[:], Mb[:], ALU.add), reads=[rL, Mb], writes=[rL])
                            mk.op(dve, lambda: V.reciprocal(rL[:], rL[:]), reads=[rL], writes=[rL])
                            mk.op(dve, lambda: V.tensor_tensor(ocb[:], O[:], rL[:], ALU.mult), reads=[O, rL], writes=[ocb])
                        yb_o, y_s = ybo[j % 2], yss[j % 2]
                        mk.op(dve, lambda: V.scalar_tensor_tensor(od[:], oc[1][:], neglam[:, l:l + 1], oc[0][:], ALU.mult, ALU.add),
                              reads=[oc[0], oc[1], neglam], writes=[od])
                        mk.op(act, lambda: A.activation(osq[:], od[:], AF.Square), reads=[od], writes=[osq])

                        def part_b(h=h, qt=qt, yb_o=yb_o, y_s=y_s):
                            mk.mm(Mb[:], [(onesm[:], osq[:])], reads=[onesm, osq], writes=[Mb])
                            mk.op(act, lambda: A.activation(rst[:], Mb[:], AF.Ln, bias=epscol[:]), reads=[Mb, epscol], writes=[rst])
                            mk.op(act, lambda: A.activation(rst[:], rst[:], AF.Exp, scale=-0.5), reads=[rst], writes=[rst])
                            mk.op(dve, lambda: V.scalar_tensor_tensor(yb_o[:], od[:], gsub[:, l:l + 1], rst[:], ALU.mult, ALU.mult),
                                  reads=[od, gsub, rst], writes=[yb_o])
                            mk.dma(sp, y_s, ybT[h * 128:(h + 1) * 128, qt * TT:(qt + 1) * TT], yb_o[:], reads=[yb_o])

                        deferred.append([3, part_b])
                    while deferred:
                        deferred.pop(0)[1]()
                    mk.barrier()
                    recycle()

                stage(6)
                with contextlib.ExitStack() as ph:
                    wbra = sb(ph, [128, 4, 1024], BF16, "wbra")
                    wbrb = sb(ph, [128, 8, 1024], BF16, "wbrb")
                    wbrc = sb(ph, [128, 4, 1024], BF16, "wbrc")
                    wout = sb(ph, [128, 8, 1024], BF16, "wout")
                    poolw = sb(ph, [128, 4, 128], BF16, "poolw")
                    gtab = sb(ph, [128, D], F32, "gtab")
                    btab = sb(ph, [128, D], F32, "btab")
                    ws = slot("p3w")
                    mk.dma_group(sp, ws, [(wbra[:], wbra_b[l]), (wbrb[:], wbrb_b[l]), (wbrc[:], wbrc_b[l]), (wout[:], wout_b[l]),
                                          (poolw[:], poolw_b[l]), (gtab[:], ln_g[l].partition_broadcast(128)),
                                          (btab[:], ln_b[l].partition_broadcast(128))], [wbra, wbrb, wbrc, wout, poolw, gtab, btab])
                    NB = 6
                    ring = [sb(ph, [128, KC, 128], BF16, "ring") for _ in range(NB)]
                    rsl = [slot(f"p3r{i}") for i in range(NB)]
                    order = [0, 1, 2, 3, 4, 5, 6, 7] + list(range(32, 48))
                    for dc in range(8):
                        order += [48 + dc, 56 + dc, 64 + dc]
                    NS = len(order)
                    total = NT * NS
                    issued = [0]

                    def ring_fill(upto):
                        while issued[0] < min(upto, total):
                            s_ = issued[0]
                            ch = order[s_ % NS]
                            mk.dma(sp, rsl[s_ % NB], ring[s_ % NB][:], win_b[l, ch], writes=[ring[s_ % NB]])
                            issued[0] += 1

                    sq = [0]

                    def wchunk():
                        s_ = sq[0]
                        ring_fill(s_ + NB)
                        sq[0] += 1
                        return ring[s_ % NB]

                    XW = TT + 2 * HALO
                    xts = [sb(ph, [128, KC, XW], BF16, "xt") for _ in range(2)]
                    ybs = [sb(ph, [128, 8, TT], BF16, "yb") for _ in range(2)]
                    ics = [sb(ph, [128, 4, TT], F32, "ic") for _ in range(1)]
                    lsl = [slot("p3l0"), slot("p3l1")]
                    icsl = slot("p3ic")
                    xres = [sb(ph, [128, D], F32, "xres") for _ in range(1)]
                    xrs = [slot("p3x0")]
                    zb = [sb(ph, [128, D], F32, "z") for _ in range(2)]
                    zss = [slot("p3z0"), slot("p3z1")]
                    lnb = (sb(ph, [128, 12], F32, "st"), sb(ph, [128, 2], F32, "mv"), sb(ph, [128, 1], F32, "rstd"),
                           sb(ph, [128, 8, TT], BF16, "xTo"), slot("p3t"))
                    U = [sb(ph, [128, XW], F32, "U") for _ in range(4)]
                    w2 = sb(ph, [128, XW], F32, "w2")
                    w4 = sb(ph, [128, XW], F32, "w4")
                    w8 = sb(ph, [128, XW], F32, "w8")
                    w16 = sb(ph, [128, XW], F32, "w16")
                    dT = [sb(ph, [128, TT], BF16, "dT") for _ in range(2)]
                    NSIL = 2
                    sil = [sb(ph, [128, TT], F32, "sil") for _ in range(NSIL)]
                    yain = sb(ph, [128, 4, TT], BF16, "yain")
                    ycin = sb(ph, [128, 4, TT], BF16, "ycin")
                    pm = [sb(ph, [128, TT], BF16, "pm") for _ in range(2)]
                    rl3 = sb(ph, [128, TT], F32, "rl3")
                    NGT = 3
                    gt = [sb(ph, [128, TT], F32, "gt") for _ in range(NGT)]
                    ma = [sb(ph, [128, TT], F32, "ma") for _ in range(3)]
                    mrg = sb(ph, [128, 8, TT], BF16, "mrg")
                    dst_xs = y_out if last else xs[nxt]
                    dst_xT = None if last else xT[nxt]

                    def p3_load(ti):
                        i = ti % 2
                        t0 = ti * TT
                        lo = max(t0 - HALO, 0)
                        hi = min(t0 + TT + HALO, T)
                        c0 = lo - (t0 - HALO)
                        mk.dma_group(sp, lsl[i], [(xts[i][:, :, c0:c0 + (hi - lo)], xT[cur][:, lo:hi].rearrange("(kc p) t -> p kc t", p=128)),
                                                  (ybs[i][:], ybT[:, t0:t0 + TT].rearrange("(h p) t -> p h t", p=128))], [xts[i], ybs[i]])

                    for i in range(2):
                        mk.op(pool, lambda: G.memset(xts[i][:], 0.0), writes=[xts[i]])
                    p3_load(0)
                    gi = 0
                    si = 0
                    for ti in range(NT):
                        if ti + 1 < NT:
                            p3_load(ti + 1)
                        i = ti % 2
                        mk.dma_group(sp, icsl, [(ics[0][:, g, :], invc_d[g, ti * TT:(ti + 1) * TT].partition_broadcast(128)) for g in range(4)], [ics[0]])
                        xt, ybt, ic = xts[i], ybs[i], ics[0]
                        unit = ti // TPU
                        first_in_unit = (ti % TPU == 0)
                        last_in_unit = (ti % TPU == TPU - 1)
                        xm = lambda kc: xt[:, kc, HALO:HALO + TT]
                        for g in range(4):
                            wc = wchunk()
                            bk = mk.bank()
                            mk.mm(bk[:], [(wc[:, kc, :], xm(kc)) for kc in range(KC)], reads=[wc, xt], writes=[bk])
                            bh = mk.bank()
                            mk.mm(bh[:, 0:HALO], [(wc[:, kc, :], xt[:, kc, 0:HALO]) for kc in range(KC)], reads=[wc, xt], writes=[bh], sig=False)
                            mk.mm(bh[:, HALO:2 * HALO], [(wc[:, kc, :], xt[:, kc, HALO + TT:XW]) for kc in range(KC)], reads=[wc, xt], writes=[bh])
                            Ug = U[g]
                            mk.op(act, lambda: A.copy(Ug[:, HALO:HALO + TT], bk[:]), reads=[bk], writes=[Ug])
                            if first_in_unit and unit == 1:
                                mk.op(dve, lambda: V.tensor_scalar_mul(Ug[:, 0:HALO], bh[:, 0:HALO], flags[:, 0:1]), reads=[bh, flags], writes=[Ug])
                            elif first_in_unit:
                                mk.op(dve, lambda: V.memset(Ug[:, 0:HALO], 0.0), writes=[Ug])
                            else:
                                mk.op(dve, lambda: V.tensor_copy(Ug[:, 0:HALO], bh[:, 0:HALO]), reads=[bh], writes=[Ug])
                            if last_in_unit and unit == 0:
                                mk.op(dve, lambda: V.tensor_scalar_mul(Ug[:, HALO + TT:XW], bh[:, HALO:2 * HALO], flags[:, 0:1]), reads=[bh, flags], writes=[Ug])
                            elif last_in_unit:
                                mk.op(dve, lambda: V.memset(Ug[:, HALO + TT:XW], 0.0), reads=[bh], writes=[Ug])
                            else:
                                mk.op(dve, lambda: V.tensor_copy(Ug[:, HALO + TT:XW], bh[:, HALO:2 * HALO]), reads=[bh], writes=[Ug])
                        for g in range(4):
                            Ug = U[g]
                            mk.op(dve, lambda: V.tensor_tensor(w2[:, 1:XW], Ug[:, 0:XW - 1], Ug[:, 1:XW], ALU.add), reads=[Ug], writes=[w2])
                            wsum = w2
                            if g >= 1:
                                mk.op(dve, lambda: V.tensor_tensor(w4[:, 2:XW - 1], w2[:, 1:XW - 2], w2[:, 3:XW], ALU.add), reads=[w2], writes=[w4])
                                wsum = w4
                            if g >= 2:
                                mk.op(dve, lambda: V.tensor_tensor(w8[:, 4:XW - 3], w4[:, 2:XW - 5], w4[:, 6:XW - 1], ALU.add), reads=[w4], writes=[w8])
                                wsum = w8
                            if g >= 3:
                                mk.op(dve, lambda: V.tensor_tensor(w16[:, 8:XW - 7], w8[:, 4:XW - 11], w8[:, 12:XW - 3], ALU.add), reads=[w8], writes=[w16])
                                wsum = w16
                            mk.op(pool, lambda: G.tensor_tensor(wsum[:, HALO:HALO + TT], wsum[:, HALO:HALO + TT], ic[:, g, :], ALU.mult), reads=[wsum, ic], writes=[wsum])
                            d_ = dT[g % 2]
                            mk.op(pool, lambda: G.tensor_tensor(d_[:], wsum[:, HALO:HALO + TT], Ug[:, HALO:HALO + TT], ALU.subtract), reads=[wsum, Ug], writes=[d_])
                            wc = wchunk()
                            bz = mk.bank()
                            mk.mm(bz[:], [(wc[:, kc, :], xm(kc)) for kc in range(KC)], reads=[wc, xt], writes=[bz])
                            sl_ = sil[si % NSIL]
                            si += 1
                            mk.op(act, lambda: A.activation(sl_[:], bz[:], AF.Silu), reads=[bz], writes=[sl_])
                            bp = mk.bank()
                            mk.mm(bp[:], [(poolw[:, g, :], d_[:])], reads=[poolw, d_], writes=[bp])
                            mk.op(dve, lambda: V.scalar_tensor_tensor(yain[:, g, :], bp[:], pscale[:, l, g:g + 1], sl_[:], ALU.mult, ALU.mult),
                                  reads=[bp, pscale, sl_], writes=[yain])
                        for h in range(8):
                            wc = wchunk()
                            bz = mk.bank()
                            mk.mm(bz[:], [(wc[:, kc, :], xm(kc)) for kc in range(KC)], reads=[wc, xt], writes=[bz])
                            sl_ = sil[si % NSIL]
                            si += 1
                            mk.op(act, lambda: A.activation(sl_[:], bz[:], AF.Silu), reads=[bz], writes=[sl_])
                            mk.op(pool, lambda: G.tensor_tensor(ybt[:, h, :], ybt[:, h, :], sl_[:], ALU.mult), reads=[ybt, sl_], writes=[ybt])
                        for h in range(4):
                            wc = wchunk()
                            bz = mk.bank()
                            mk.mm(bz[:], [(wc[:, kc, :], xm(kc)) for kc in range(KC)], reads=[wc, xt], writes=[bz])
                            mk.op(act, lambda: A.copy(mrg[:, h, :], bz[:]), reads=[bz], writes=[mrg])
                        for h in range(4):
                            wc = wchunk()
                            bz = mk.bank()
                            mk.mm(bz[:], [(wc[:, kc, :], xm(kc)) for kc in range(KC)], reads=[wc, xt], writes=[bz])
                            sl_ = sil[si % NSIL]
                            si += 1
                            mk.op(act, lambda: A.activation(sl_[:], bz[:], AF.Silu), reads=[bz], writes=[sl_])
                            bo = mk.bank()
                            bl = mk.bank()
                            for mb in range(2):
                                bs_ = mk.bank()
                                mk.mm(bs_[:], [(kmT[:, unit, h, mb * 128:(mb + 1) * 128], mrg[:, h, :])], reads=[kmT, mrg], writes=[bs_])
                                pm_ = pm[mb]
                                mk.op(act, lambda: A.activation(pm_[:], bs_[:], AF.Exp, scale=float(128 ** -0.5)), reads=[bs_], writes=[pm_])
                                mk.mm(bo[:], [(vm[:, unit, mb, h * 128:(h + 1) * 128], pm_[:])], reads=[vm, pm_], writes=[bo], start=(mb == 0), stop=(mb == 1))
                                mk.mm(bl[:], [(ones[:], pm_[:])], reads=[ones, pm_], writes=[bl], start=(mb == 0), stop=(mb == 1))
                            mk.op(dve, lambda: V.reciprocal(rl3[:], bl[:]), reads=[bl], writes=[rl3])
                            mk.op(dve, lambda: V.tensor_tensor(rl3[:], rl3[:], sl_[:], ALU.mult), reads=[rl3, sl_], writes=[rl3])
                            mk.op(dve, lambda: V.tensor_tensor(ycin[:, h, :], bo[:], rl3[:], ALU.mult), reads=[bo, rl3], writes=[ycin])
                        for dc in range(8):
                            cs_ = slice(dc * 128, (dc + 1) * 128)
                            gts = []
                            for jg in range(3):
                                wc = wchunk()
                                bg = mk.bank()
                                mk.mm(bg[:], [(wc[:, kc, :], xm(kc)) for kc in range(KC)], reads=[wc, xt], writes=[bg])
                                g_ = gt[gi % NGT]
                                gi += 1
                                mk.op(act, lambda: A.activation(g_[:], bg[:], AF.Sigmoid), reads=[bg], writes=[g_])
                                gts.append(g_)
                            ba = mk.bank()
                            mk.mm(ba[:], [(wbra[:, g, cs_], yain[:, g, :]) for g in range(4)], reads=[wbra, yain], writes=[ba])
                            mk.op(dve, lambda: V.tensor_tensor(ma[0][:], ba[:], gts[0][:], ALU.mult), reads=[ba, gts[0]], writes=[ma[0]])
                            bb = mk.bank()
                            mk.mm(bb[:], [(wbrb[:, h, cs_], ybt[:, h, :]) for h in range(8)], reads=[wbrb, ybt], writes=[bb])
                            mk.op(dve, lambda: V.tensor_tensor(ma[1][:], bb[:], gts[1][:], ALU.mult), reads=[bb, gts[1]], writes=[ma[1]])
                            bc = mk.bank()
                            mk.mm(bc[:], [(wbrc[:, h, cs_], ycin[:, h, :]) for h in range(4)], reads=[wbrc, ycin], writes=[bc])
                            mk.op(dve, lambda: V.tensor_tensor(ma[2][:], bc[:], gts[2][:], ALU.mult), reads=[bc, gts[2]], writes=[ma[2]])
                            mk.op(pool, lambda: G.tensor_tensor(ma[0][:], ma[0][:], ma[1][:], ALU.add), reads=[ma[0], ma[1]], writes=[ma[0]])
                            mk.op(pool, lambda: G.tensor_tensor(mrg[:, dc, :], ma[0][:], ma[2][:], ALU.add), reads=[ma[0], ma[2]], writes=[mrg])
                        for b in range(4):
                            xr = xres[0]
                            z = zb[b % 2]
                            mk.dma(sp, xrs[0], xr[:], xs[cur][ti * TT + b * 128: ti * TT + (b + 1) * 128, :], writes=[xr])
                            for half in range(2):
                                bo = mk.bank()
                                mk.mm(bo[:], [(mrg[:, dc, b * 128:(b + 1) * 128], wout[:, dc, half * 512:(half + 1) * 512]) for dc in range(8)],
                                      reads=[mrg, wout], writes=[bo])
                                mk.op(dve, lambda: V.scalar_tensor_tensor(z[:, half * 512:(half + 1) * 512], xr[:, half * 512:(half + 1) * 512],
                                                                          float((2.0 * 4) ** 0.25), bo[:], ALU.mult, ALU.add),
                                      reads=[xr, bo], writes=[z])
                            ln_block(lnb, ti * TT + b * 128, b, z, zss[b % 2], dst_xs, dst_xT is not None, gtab, btab)
                        if dst_xT is not None:
                            ln_flush(lnb, ti, dst_xT)
                    mk.barrier()
                    recycle()
        except _Stop:
            pass
        mk.dead = False
        mk.barrier()
        recycle()
    return nc


_CACHE = {}


def _core_assign(c, x_prompt, x_sample, mem_prompt, mem_sample):
    if c < 4:
        x = np.concatenate([x_prompt[c], x_sample[c]], axis=0)
        m = np.concatenate([mem_prompt[c], mem_prompt[c], mem_sample[c]], axis=0)
    else:
        s0 = 4 + 3 * (c - 4)
        x = np.concatenate([x_sample[s0], x_sample[s0 + 1], x_sample[s0 + 2]], axis=0)
        m = np.concatenate([mem_sample[s0], mem_sample[s0 + 1], mem_sample[s0 + 2]], axis=0)
    return np.ascontiguousarray(x), np.ascontiguousarray(m)


def kernel(x_prompt, x_sample, mem_prompt, mem_sample, ln_in_g, ln_in_b, w_in, w_mem_kv,
           pool_w, pool_scale, lam_q1, lam_k1, lam_q2, lam_k2, subln_g, w_br_a, w_br_b,
           w_br_c, w_out, ln_g, ln_b):
    f = lambda a: np.ascontiguousarray(np.asarray(a, dtype=np.float32))
    x_prompt, x_sample, mem_prompt, mem_sample = f(x_prompt), f(x_sample), f(mem_prompt), f(mem_sample)
    UL = x_sample.shape[1]
    DEPTH = w_in.shape[0]
    assert x_prompt.shape[1] == 2 * UL and x_prompt.shape[0] == 4 and x_sample.shape[0] == 16
    key = (UL, DEPTH)
    if key not in _CACHE:
        _CACHE[key] = build(UL, DEPTH)
    nc = _CACHE[key]
    rT, ident, ones, onesm = _const_tables()
    shared = {"ln_in_g": f(ln_in_g), "ln_in_b": f(ln_in_b), "w_in": f(w_in), "w_mem_kv": f(w_mem_kv),
              "pool_w": f(pool_w), "pool_scale": f(pool_scale), "lam_q1": f(lam_q1), "lam_k1": f(lam_k1),
              "lam_q2": f(lam_q2), "lam_k2": f(lam_k2), "subln_g": f(subln_g), "w_br_a": f(w_br_a),
              "w_br_b": f(w_br_b), "w_br_c": f(w_br_c), "w_out": f(w_out), "ln_g": f(ln_g), "ln_b": f(ln_b),
              "rT": rT, "ident": ident, "ones": ones, "onesm": onesm}
    in_maps = []
    for c in range(NCORES):
        x, m = _core_assign(c, x_prompt, x_sample, mem_prompt, mem_sample)
        ropeC, ropeS, invc, flags = _host_tables(UL, c)
        d = dict(shared)
        d.update({"x": x, "mem": m, "ropeC": ropeC, "ropeS": ropeS, "invc": invc, "flags": flags})
        in_maps.append(d)
    res = run_bass_kernel_spmd(nc, in_maps, core_ids=list(range(NCORES)))
    y_prompt = np.empty_like(x_prompt)
    y_sample = np.empty_like(x_sample)
    for c in range(NCORES):
        y = np.asarray(res.results[c]["y"], dtype=np.float32)
        if c < 4:
            y_prompt[c] = y[:2 * UL]
            y_sample[c] = y[2 * UL:]
        else:
            s0 = 4 + 3 * (c - 4)
            for k in range(3):
                y_sample[s0 + k] = y[k * UL:(k + 1) * UL]
    return (y_prompt, y_sample)
```
